# Optimizing a Trainium2 kernel written in Bass

```python
import math
import jax, jax.numpy as jnp
from jax import lax
import numpy as np


D_MODEL = 2048
BATCH = 4
SEQ = 8192
DEPTH = 1

MLA_HEADS = 8
MLA_Q_RANK = 768
MLA_KV_RANK = 512
MLA_NOPE = 128
MLA_ROPE = 64
MLA_QK = MLA_NOPE + MLA_ROPE
MLA_V = 128
MLA_WIDTH = MLA_HEADS * MLA_V
ROPE_THETA = 10000.0
Q_BLOCK = 128

RWKV_HEAD = 64
RWKV_WIDTH = D_MODEL - MLA_WIDTH
RWKV_HEADS = RWKV_WIDTH // RWKV_HEAD
DECAY_LORA = max(32, int(round(1.8 * math.sqrt(RWKV_WIDTH) / 32)) * 32)
AAA_LORA = max(32, int(round(1.8 * math.sqrt(RWKV_WIDTH) / 32)) * 32)
GATE_LORA = max(32, int(round(0.6 * RWKV_WIDTH ** 0.8 / 32)) * 32)
RWKV_SPLITS = (RWKV_WIDTH, RWKV_WIDTH, RWKV_WIDTH, DECAY_LORA, AAA_LORA, GATE_LORA)
RWKV_IN = 3 * RWKV_WIDTH + DECAY_LORA + AAA_LORA + GATE_LORA

IN_SPLITS = (MLA_Q_RANK, MLA_KV_RANK, MLA_ROPE, RWKV_IN)
IN_WIDTH = MLA_Q_RANK + MLA_KV_RANK + MLA_ROPE + RWKV_IN

FFN_HIDDEN = ((8 * D_MODEL + 3 * 256 - 1) // (3 * 256)) * 256

NORM_EPS = 1e-6
GN_EPS = 64e-5

kernel_name = 'hybrid_mla_rwkv7_block'


def _split(t, sizes):
    idx = np.cumsum(np.array(sizes))[:-1].tolist()
    return jnp.split(t, idx, axis=-1)


def rms_norm(x, g, eps=NORM_EPS):
    xf = x.astype(jnp.float32)
    y = xf * lax.rsqrt(jnp.mean(xf * xf, axis=-1, keepdims=True) + eps)
    return (y * g.astype(jnp.float32)).astype(x.dtype)


def rope_tables(S, dim):
    inv_freq = 1.0 / (ROPE_THETA ** (jnp.arange(0, dim, 2, dtype=jnp.float32) / dim))
    ang = jnp.arange(S, dtype=jnp.float32)[:, None] * inv_freq[None, :]
    return jnp.cos(ang), jnp.sin(ang)


def apply_rope(x, cos, sin):
    xf = x.astype(jnp.float32)
    x1, x2 = jnp.split(xf, 2, axis=-1)
    out = jnp.concatenate([x1 * cos - x2 * sin, x2 * cos + x1 * sin], axis=-1)
    return out.astype(x.dtype)


def mla_group(q_lat, kv_lat, k_rope, q_lat_norm, w_uq, kv_lat_norm, w_ukv,
              q_head_norm, k_nope_norm, k_rope_norm):
    B, S, _ = q_lat.shape
    q = (rms_norm(q_lat, q_lat_norm) @ w_uq).reshape(B, S, MLA_HEADS, MLA_QK)
    q = rms_norm(q, q_head_norm)
    q_nope, q_pe = q[..., :MLA_NOPE], q[..., MLA_NOPE:]
    kv = (rms_norm(kv_lat, kv_lat_norm) @ w_ukv).reshape(B, S, MLA_HEADS, MLA_NOPE + MLA_V)
    k_nope = rms_norm(kv[..., :MLA_NOPE], k_nope_norm)
    v = kv[..., MLA_NOPE:]
    k_pe = rms_norm(k_rope, k_rope_norm)
    cos, sin = rope_tables(S, MLA_ROPE)
    q_pe = apply_rope(q_pe, cos[:, None, :], sin[:, None, :])
    k_pe = apply_rope(k_pe, cos, sin)
    n_blk = S // Q_BLOCK

    def to_blocks(t):
        return t.reshape(B, n_blk, Q_BLOCK, MLA_HEADS, t.shape[-1]).swapaxes(0, 1)

    key_pos = jnp.arange(S)
    scale = MLA_QK ** -0.5

    def attend_block(args):
        qn, qp, blk = args
        s = (jnp.einsum('bqhd,bkhd->bhqk', qn, k_nope)
             + jnp.einsum('bqhd,bkd->bhqk', qp, k_pe))
        s = s.astype(jnp.float32) * scale
        q_pos = blk * Q_BLOCK + jnp.arange(Q_BLOCK)
        s = jnp.where(key_pos[None, :] <= q_pos[:, None], s, -jnp.inf)
        p = jax.nn.softmax(s, axis=-1).astype(v.dtype)
        return jnp.einsum('bhqk,bkhd->bqhd', p, v)

    o = lax.map(attend_block, (to_blocks(q_nope), to_blocks(q_pe), jnp.arange(n_blk)))
    return o.swapaxes(0, 1).reshape(B, S, MLA_WIDTH)


def rwkv7_group(p, shift_mix, w0, w2, a0, a2, g2, k_k, k_a, r_k, ln_g, ln_b):
    B, S, _ = p.shape
    f32 = jnp.float32
    shifted = jnp.pad(p[:, :-1], ((0, 0), (1, 0), (0, 0)))
    p = p + (shifted - p) * shift_mix
    r, k, v, xw, xa, xg = _split(p, RWKV_SPLITS)
    w_log = -jax.nn.softplus(-(w0 + jnp.tanh(xw) @ w2).astype(f32)) - 0.5
    decay = jnp.exp(-jnp.exp(w_log))
    a = jax.nn.sigmoid((a0 + xa @ a2).astype(f32))
    g = jax.nn.sigmoid(xg) @ g2

    def heads(t):
        return t.astype(f32).reshape(B, S, RWKV_HEADS, RWKV_HEAD)

    r, k, v, decay, a = heads(r), heads(k), heads(v), heads(decay), heads(a)
    kk = k * k_k.astype(f32).reshape(RWKV_HEADS, RWKV_HEAD)
    kk = kk * lax.rsqrt(jnp.maximum(jnp.sum(kk * kk, axis=-1, keepdims=True), 1e-24))
    k = k * (1.0 + (a - 1.0) * k_a.astype(f32).reshape(RWKV_HEADS, RWKV_HEAD))
    xs = tuple(t.swapaxes(0, 1) for t in (r, decay, k, v, -kk, kk * a))

    def step(state, inp):
        r_t, w_t, k_t, v_t, a_t, b_t = inp
        sa = jnp.einsum('bhij,bhj->bhi', state, a_t)
        state = (state * w_t[:, :, None, :]
                 + sa[..., None] * b_t[:, :, None, :]
                 + v_t[..., None] * k_t[:, :, None, :])
        return state, jnp.einsum('bhij,bhj->bhi', state, r_t)

    state0 = jnp.zeros((B, RWKV_HEADS, RWKV_HEAD, RWKV_HEAD), f32)
    _, y = lax.scan(step, state0, xs)
    y = y.swapaxes(0, 1)
    mu = jnp.mean(y, axis=-1, keepdims=True)
    var = jnp.mean(jnp.square(y - mu), axis=-1, keepdims=True)
    yn = ((y - mu) * lax.rsqrt(var + GN_EPS)).reshape(B, S, RWKV_WIDTH)
    yn = yn * ln_g.astype(f32) + ln_b.astype(f32)
    bonus = jnp.sum(r * k * r_k.astype(f32), axis=-1, keepdims=True) * v
    out = (yn + bonus.reshape(B, S, RWKV_WIDTH)) * g.astype(f32)
    return out.astype(p.dtype)


def setup_inputs(seed: int = 0) -> dict:
    key = jax.random.key(seed)
    ks = jax.random.split(key, 27)
    f32 = jnp.float32
    L = DEPTH

    def nrm(k, shape, scale):
        return jax.random.normal(k, shape, f32) * scale

    def gain(k, shape):
        return 1.0 + 0.02 * jax.random.normal(k, shape, f32)

    return {
        'x': jax.random.normal(ks[0], (BATCH, SEQ, D_MODEL), f32),
        'attn_norm_g': gain(ks[1], (L, D_MODEL)),
        'w_in': nrm(ks[2], (L, D_MODEL, IN_WIDTH), D_MODEL ** -0.5),
        'q_lat_norm': gain(ks[3], (L, MLA_Q_RANK)),
        'w_uq': nrm(ks[4], (L, MLA_Q_RANK, MLA_HEADS * MLA_QK), MLA_Q_RANK ** -0.5),
        'kv_lat_norm': gain(ks[5], (L, MLA_KV_RANK)),
        'w_ukv': nrm(ks[6], (L, MLA_KV_RANK, MLA_HEADS * (MLA_NOPE + MLA_V)), MLA_KV_RANK ** -0.5),
        'q_head_norm': gain(ks[7], (L, MLA_QK)),
        'k_nope_norm': gain(ks[8], (L, MLA_NOPE)),
        'k_rope_norm': gain(ks[9], (L, MLA_ROPE)),
        'rwkv_shift_mix': jax.random.uniform(ks[10], (L, RWKV_IN), f32),
        'rwkv_w0': jax.random.uniform(ks[11], (L, RWKV_WIDTH), f32, -6.0, -1.0),
        'rwkv_w2': nrm(ks[12], (L, DECAY_LORA, RWKV_WIDTH), 0.1 * DECAY_LORA ** -0.5),
        'rwkv_a0': nrm(ks[13], (L, RWKV_WIDTH), 0.1),
        'rwkv_a2': nrm(ks[14], (L, AAA_LORA, RWKV_WIDTH), 0.5 * AAA_LORA ** -0.5),
        'rwkv_g2': nrm(ks[15], (L, GATE_LORA, RWKV_WIDTH), GATE_LORA ** -0.5),
        'rwkv_k_k': 0.85 + 0.05 * jax.random.normal(ks[16], (L, RWKV_WIDTH), f32),
        'rwkv_k_a': 1.0 + 0.05 * jax.random.normal(ks[17], (L, RWKV_WIDTH), f32),
        'rwkv_r_k': nrm(ks[18], (L, RWKV_HEADS, RWKV_HEAD), 0.1),
        'rwkv_ln_g': gain(ks[19], (L, RWKV_WIDTH)),
        'rwkv_ln_b': nrm(ks[20], (L, RWKV_WIDTH), 0.02),
        'w_out': nrm(ks[21], (L, D_MODEL, D_MODEL), D_MODEL ** -0.5),
        'ffn_norm_g': gain(ks[22], (L, D_MODEL)),
        'w_gate': nrm(ks[23], (L, D_MODEL, FFN_HIDDEN), D_MODEL ** -0.5),
        'w_up': nrm(ks[24], (L, D_MODEL, FFN_HIDDEN), D_MODEL ** -0.5),
        'w_down': nrm(ks[25], (L, FFN_HIDDEN, D_MODEL), FFN_HIDDEN ** -0.5),
    }


def reference(x, attn_norm_g, w_in, q_lat_norm, w_uq, kv_lat_norm, w_ukv,
              q_head_norm, k_nope_norm, k_rope_norm, rwkv_shift_mix, rwkv_w0,
              rwkv_w2, rwkv_a0, rwkv_a2, rwkv_g2, rwkv_k_k, rwkv_k_a, rwkv_r_k,
              rwkv_ln_g, rwkv_ln_b, w_out, ffn_norm_g, w_gate, w_up, w_down):
    for l in range(DEPTH):
        h = rms_norm(x, attn_norm_g[l])
        proj = h @ w_in[l]
        q_lat, kv_lat, k_rope, p_rwkv = _split(proj, IN_SPLITS)
        out_a = mla_group(q_lat, kv_lat, k_rope, q_lat_norm[l], w_uq[l], kv_lat_norm[l],
                          w_ukv[l], q_head_norm[l], k_nope_norm[l], k_rope_norm[l])
        out_b = rwkv7_group(p_rwkv, rwkv_shift_mix[l], rwkv_w0[l], rwkv_w2[l], rwkv_a0[l],
                            rwkv_a2[l], rwkv_g2[l], rwkv_k_k[l], rwkv_k_a[l], rwkv_r_k[l],
                            rwkv_ln_g[l], rwkv_ln_b[l])
        x = x + jnp.concatenate([out_a, out_b], axis=-1) @ w_out[l]
        h = rms_norm(x, ffn_norm_g[l])
        x = x + (jax.nn.silu(h @ w_gate[l]) * (h @ w_up[l])) @ w_down[l]
    return x
```

```python
import numpy as np
import concourse.bass as bass
import concourse.mybir as mybir

F32 = mybir.dt.float32
BF16 = mybir.dt.bfloat16
I32 = mybir.dt.int32
AF = mybir.ActivationFunctionType
ALU = mybir.AluOpType
AX = mybir.AxisListType

COMPUTE = ("pe", "act", "dve", "pool")
NDMA_SEM = 12


class Buf:
    __slots__ = ("name", "last_w", "readers")

    def __init__(self, name=""):
        self.name = name
        self.last_w = None
        self.readers = []


class Op:
    __slots__ = ("eng", "fn", "idx", "deps", "signal", "is_dma", "dma_k", "cnt")

    def __init__(self, eng, fn, idx, is_dma):
        self.eng = eng
        self.fn = fn
        self.idx = idx
        self.deps = []
        self.signal = False
        self.is_dma = is_dma
        self.dma_k = -1
        self.cnt = 0


class Sched:
    def __init__(self, nc):
        self.nc = nc
        self.engs = ("sp", "act", "dve", "pe", "pool")
        self.ops = {e: [] for e in self.engs}
        self.nops = {e: 0 for e in self.engs}
        self.sigcnt = {e: 0 for e in self.engs}
        self.dmacnt = {e: 0 for e in self.engs}
        self.seen = {e: {o: -1 for o in self.engs} for e in self.engs}
        self.seen_dma = {e: set() for e in self.engs}
        self.csem = {}
        self.dsem = {}
        self.pending_dma = []
        self.all_dma = {e: [] for e in self.engs}

    def alloc_sems(self, stack):
        for e in COMPUTE:
            self.csem[e] = stack.enter_context(self.nc.semaphore("c_" + e))
        for q in ("sp", "pool", "act"):
            self.dsem[q] = [stack.enter_context(self.nc.semaphore("d_%s%d" % (q, i))) for i in range(NDMA_SEM)]

    def _add(self, eng, fn, reads, writes, is_dma):
        o = Op(eng, fn, self.nops[eng], is_dma)
        self.nops[eng] += 1
        deps = {}
        for b in reads:
            if b.last_w is not None:
                deps[id(b.last_w)] = b.last_w
        for b in writes:
            if b.last_w is not None:
                deps[id(b.last_w)] = b.last_w
            for r in b.readers:
                deps[id(r)] = r
        best = {}
        for d in deps.values():
            if d is o:
                continue
            if d.is_dma:
                if id(d) in self.seen_dma[eng]:
                    continue
                self.seen_dma[eng].add(id(d))
                o.deps.append(d)
            else:
                if d.eng == eng and eng == "pe":
                    continue
                if d.idx <= self.seen[eng][d.eng]:
                    continue
                if d.eng not in best or best[d.eng].idx < d.idx:
                    best[d.eng] = d
        for d in best.values():
            self.seen[eng][d.eng] = d.idx
            d.signal = True
            o.deps.append(d)
        for b in reads:
            b.readers.append(o)
        for b in writes:
            b.last_w = o
            b.readers = []
        if is_dma:
            o.dma_k = self.dmacnt[eng]
            self.dmacnt[eng] += 1
            self.all_dma[eng].append(o)
        self.ops[eng].append(o)
        return o

    def op(self, eng, fn, reads=(), writes=()):
        return self._add(eng, fn, reads, writes, False)

    def dma(self, q, fn, reads=(), writes=()):
        return self._add(q, fn, reads, writes, True)

    def barrier(self):
        lasts = {}
        for e in COMPUTE:
            for o in reversed(self.ops[e]):
                if (not o.is_dma) and o.fn is not None:
                    lasts[e] = o
                    break
        dmas = []
        for q in self.engs:
            n = len(self.all_dma[q])
            dmas.extend(self.all_dma[q][max(0, n - NDMA_SEM):])
        for e in self.engs:
            o = Op(e, None, self.nops[e], False)
            self.nops[e] += 1
            for e2, d in lasts.items():
                if e2 == e:
                    continue
                if d.idx > self.seen[e][e2]:
                    self.seen[e][e2] = d.idx
                    d.signal = True
                    o.deps.append(d)
            for d in dmas:
                if id(d) not in self.seen_dma[e]:
                    self.seen_dma[e].add(id(d))
                    o.deps.append(d)
            self.ops[e].append(o)

    def _emit_engine(self, ename, e):
        for o in self.ops[ename]:
            for d in o.deps:
                if d.is_dma:
                    sem = self.dsem[d.eng][d.dma_k % NDMA_SEM]
                    e.wait_ge(sem, 16 * (d.dma_k // NDMA_SEM + 1))
                else:
                    e.wait_ge(self.csem[d.eng], d.cnt)
            if o.fn is None:
                continue
            if o.is_dma:
                k = o.dma_k
                sem = self.dsem[ename][k % NDMA_SEM]
                if k >= NDMA_SEM:
                    e.wait_ge(sem, 16 * (k // NDMA_SEM))
                ins = o.fn(e)
                ins.then_inc(sem, 16)
            else:
                ins = o.fn(e)
                if o.signal:
                    ins.then_inc(self.csem[ename], 1)

    def emit(self, name=None):
        for e in COMPUTE:
            c = self.sigcnt[e]
            for o in self.ops[e]:
                if (not o.is_dma) and o.signal:
                    c += 1
                    o.cnt = c
            self.sigcnt[e] = c
        with self.nc.Block() as block:
            @block.sync
            def _(e):
                self._emit_engine("sp", e)

            @block.scalar
            def _(e):
                self._emit_engine("act", e)

            @block.vector
            def _(e):
                self._emit_engine("dve", e)

            @block.tensor
            def _(e):
                self._emit_engine("pe", e)

            @block.gpsimd
            def _(e):
                self._emit_engine("pool", e)
        self.ops = {e: [] for e in self.engs}


import contextlib
import math
import ml_dtypes
from concourse.bass_utils import run_bass_kernel_spmd

D = 2048
NH = 8
QR, KVR, ROPE, NOPE, DV = 768, 512, 64, 128, 128
QK = NOPE + ROPE
RW = 1024
RH, RN = 16, 64
RIN = 3360
INW = 4704
FF = 5632
EPS = 1e-6
GN_EPS = 64e-5
TS = 512
CDEC = -math.exp(-0.5)


def own_tiles(S, c):
    nt = S // TS
    out = []
    for j in range(nt // 2):
        first = (j % 2 == 0)
        if c == 1:
            first = not first
        out.append(2 * j if first else 2 * j + 1)
    return out


class Ctx:
    pass


def build_nc(S, debug=False):
    nc = bass.Bass("TRN2", target_bir_lowering=False)
    SO = S // 2
    NT = S // TS
    NSL = NT // 2
    NCH = S // 128
    kind_dbg = "ExternalOutput" if debug else "Internal"

    def din(name, shape, dt=F32):
        return nc.dram_tensor(name, list(shape), dt, kind="ExternalInput").ap()

    def dscr(name, shape, dt, dbg=False):
        return nc.dram_tensor(name, list(shape), dt, kind=(kind_dbg if dbg else "Internal")).ap()

    g = Ctx()
    g.nc, g.S, g.SO, g.NT, g.NSL, g.NCH = nc, S, SO, NT, NSL, NCH
    g.x_true = din("x_true", [S, D])
    g.x_own = din("x_own", [SO, D])
    g.cs_true = din("cs_true", [S, 64])
    g.cs_own = din("cs_own", [SO, 64])
    g.sel = din("sel", [1, 2 * NSL])
    g.attn_norm_g = din("attn_norm_g", [1, D])
    g.w_in = din("w_in", [D, INW])
    g.q_lat_norm = din("q_lat_norm", [1, QR])
    g.w_uq = din("w_uq", [QR, NH * QK])
    g.kv_lat_norm = din("kv_lat_norm", [1, KVR])
    g.w_ukv = din("w_ukv", [KVR, NH * 256])
    g.q_head_norm = din("q_head_norm", [1, QK])
    g.k_nope_norm = din("k_nope_norm", [1, NOPE])
    g.k_rope_norm = din("k_rope_norm", [1, ROPE])
    g.shift_mix = din("rwkv_shift_mix", [1, RIN])
    g.w0 = din("rwkv_w0", [1, RW])
    g.w2 = din("rwkv_w2", [64, RW])
    g.a0 = din("rwkv_a0", [1, RW])
    g.a2 = din("rwkv_a2", [64, RW])
    g.g2 = din("rwkv_g2", [160, RW])
    g.k_k = din("rwkv_k_k", [1, RW])
    g.k_a = din("rwkv_k_a", [1, RW])
    g.r_k = din("rwkv_r_k", [1, RW])
    g.ln_g = din("rwkv_ln_g", [1, RW])
    g.ln_b = din("rwkv_ln_b", [1, RW])
    g.w_out = din("w_out", [D, D])
    g.ffn_norm_g = din("ffn_norm_g", [1, D])
    g.w_gate = din("w_gate", [D, FF])
    g.w_up = din("w_up", [D, FF])
    g.w_down = din("w_down", [FF, D])
    g.ident = din("ident", [128, 128])
    g.diagmask = din("diagmask", [128, 4 * 512])
    g.rmask_ar = din("rmask_ar", [128, 256])
    g.rmask_lo = din("rmask_lo", [128, 128])
    g.tri = din("tri", [128, 3 * 128 + 1])
    g.out = nc.dram_tensor("out", [SO, D], F32, kind="ExternalOutput").ap()
    g.wbA = dscr("wbA", [9, 128, 16 * 512], BF16)
    g.ukvb = dscr("ukvb", [128, 4 * 2048], BF16)
    g.wqb = dscr("wqb", [128, 16 * 768], BF16)
    g.uqb = dscr("uqb", [128, 6 * 1536], BF16)
    g.w2b = dscr("w2b", [64, 1024], BF16)
    g.a2b = dscr("a2b", [64, 1024], BF16)
    g.g2b = dscr("g2b", [160, 1024], BF16)
    g.wbO = dscr("wbO", [8, 128, 16 * 256], BF16)
    g.wbGU = dscr("wbGU", [44, 128, 2 * 16 * 128], BF16)
    g.wbD = dscr("wbD", [16, 128, 11 * 512], BF16)
    g.KT = dscr("KT", [NH, 128, S], BF16, True)
    g.KPT = dscr("KPT", [64, S], BF16, True)
    g.V = dscr("V", [S, NH * DV], BF16, True)
    g.QT = dscr("QT", [NH, 128, SO], BF16, True)
    g.QPT = dscr("QPT", [NH, 64, SO], BF16, True)
    g.PR = dscr("PR", [S, RIN], F32, True)
    g.AT = dscr("AT", [1024, SO], BF16, True)
    g.BT = dscr("BT", [1024, S], BF16, True)

    g.ablocks = [(768, 512), (1280, 64)] + [(1344 + 512 * i, min(512, RIN - 512 * i)) for i in range(7)]
    S_ = Sched(nc)
    g.sch = S_
    g.wb = {k: Buf(k) for k in ("wbA", "wbO", "wbGU", "wbD", "KT", "KPT", "V", "QT", "QPT", "PR", "AT", "BT")}
    g.wbA_b = [Buf("wbA%d" % i) for i in range(9)]
    with contextlib.ExitStack() as st:
        S_.alloc_sems(st)
        import os
        ph = os.environ.get("KPH", "cast,a1,a2,c,d,e").split(",")
        if "cast" in ph:
            phase_cast(g)
        if "a1" in ph:
            phase_a1(g)
        if "a2" in ph:
            phase_a2(g)
        if "c" in ph:
            phase_c(g)
        if "d" in ph:
            phase_d(g)
        if "e" in ph:
            phase_e(g)
        S_.barrier()
        S_.emit()
    return nc


def MM(out, l, r, st=True, sp=True):
    return lambda e: e.matmul(out, l, r, start=st, stop=sp)


def TR(out, in_, ident):
    return lambda e: e.transpose(out, in_, ident)


def ACTF(out, in_, func, **kw):
    return lambda e: e.activation(out=out, in_=in_, func=func, **kw)


def CPY(out, in_):
    def f(e):
        if hasattr(e, "tensor_copy"):
            return e.tensor_copy(out, in_)
        return e.activation(out=out, in_=in_, func=AF.Copy)
    return f


def TT(out, a, b, op):
    return lambda e: e.tensor_tensor(out, a, b, op)


def TSC(out, a, s1, s2, op0, op1=None):
    if op1 is None:
        return lambda e: e.tensor_scalar(out, a, s1, None, op0)
    return lambda e: e.tensor_scalar(out, a, s1, s2, op0, op1)


def STT(out, a, s, b, op0, op1):
    return lambda e: e.scalar_tensor_tensor(out, a, s, b, op0, op1)


def RED(out, in_, op=None):
    return lambda e: e.tensor_reduce(out, in_, AX.X, ALU.add if op is None else op)


def RCP(out, in_):
    return lambda e: e.reciprocal(out, in_)


def DMA(out, in_):
    return lambda e: e.dma_start(out=out, in_=in_)


def MEMSET(ap, v):
    return lambda e: e.memset(ap, v)


def rstd_ops(S_, ss, n, eps, B):
    S_.op("dve", TSC(ss, ss, 1.0 / n, eps, ALU.mult, ALU.add), reads=[B], writes=[B])
    S_.op("act", ACTF(ss, ss, AF.Sqrt), reads=[B], writes=[B])
    S_.op("dve", RCP(ss, ss), reads=[B], writes=[B])


def bc_last(ap, n):
    sh = list(ap.shape)
    return ap.unsqueeze(len(sh)).broadcast_to(sh + [n])


def bc_mid(ap, n):
    sh = list(ap.shape)
    return ap.unsqueeze(1).broadcast_to([sh[0], n] + sh[1:])


class Ring:
    def __init__(self, items):
        self.items = items
        self.i = 0

    def next(self):
        it = self.items[self.i % len(self.items)]
        self.i += 1
        return it


def phase_cast(g):
    nc, S_ = g.nc, g.sch
    jobs = []

    def blk(w, r0, nk, c0, cb):
        return w[r0:r0 + nk * 128, c0:c0 + cb].rearrange("(k p) c -> p k c", p=128)
    for b, (c0, cb) in enumerate(g.ablocks):
        jobs.append(([(blk(g.w_in, 0, 16, c0, cb), 0, 16, cb)], g.wbA[b, :, 0:16 * cb], 128, 16 * cb))
    jobs.append(([(blk(g.w_ukv, 0, 4, 0, 2048), 0, 4, 2048)], g.ukvb[:, :], 128, 8192))
    for hf in range(2):
        jobs.append(([(blk(g.w_in, 0, 16, hf * 384, 384), 0, 16, 384)],
                     g.wqb.rearrange("p (k c) -> p k c", c=768)[:, :, hf * 384:(hf + 1) * 384], 128, 16 * 384))
    for hf in range(2):
        jobs.append(([(blk(g.w_uq, 0, 6, hf * 768, 768), 0, 6, 768)],
                     g.uqb.rearrange("p (k c) -> p k c", c=1536)[:, :, hf * 768:(hf + 1) * 768], 128, 6 * 768))
    jobs.append(([(g.w2[:, :], 0, 1, 1024)], g.w2b[:, :], 64, 1024))
    jobs.append(([(g.a2[:, :], 0, 1, 1024)], g.a2b[:, :], 64, 1024))
    jobs.append(([(g.g2[0:128, :], 0, 1, 1024)], g.g2b[0:128, :], 128, 1024))
    jobs.append(([(g.g2[128:160, :], 0, 1, 1024)], g.g2b[128:160, :], 32, 1024))
    for b in range(8):
        jobs.append(([(blk(g.w_out, 0, 16, b * 256, 256), 0, 16, 256)], g.wbO[b], 128, 4096))
    for hc in range(44):
        jobs.append(([(blk(g.w_gate, 0, 16, hc * 128, 128), 0, 16, 128), (blk(g.w_up, 0, 16, hc * 128, 128), 2048, 16, 128)],
                     g.wbGU[hc], 128, 4096))
    for fb in range(4):
        for hg in range(4):
            jobs.append(([(blk(g.w_down, hg * 11 * 128, 11, fb * 512, 512), 0, 11, 512)], g.wbD[fb * 4 + hg], 128, 5632))
    with contextlib.ExitStack() as s1:
        fr = Ring([(s1.enter_context(nc.sbuf_tensor("cf%d" % i, [128, 8192], F32)), Buf()) for i in range(3)])
        br = Ring([(s1.enter_context(nc.sbuf_tensor("cb%d" % i, [128, 8192], BF16)), Buf()) for i in range(3)])
        engs = ["act", "dve", "pool"]

        def issue_load(j):
            srcs, dst, P, n = jobs[j]
            f, fB = fr.next()
            for (src, off, a, b_) in srcs:
                if a == 1:
                    S_.dma("sp", DMA(f[0:P, off:off + b_], src), writes=[fB])
                else:
                    S_.dma("sp", DMA(f[0:P, off:off + a * b_].rearrange("p (k c) -> p k c", c=b_), src), writes=[fB])
            return f, fB
        pend = [issue_load(0), issue_load(1)]
        for j in range(len(jobs)):
            srcs, dst, P, n = jobs[j]
            f, fB = pend.pop(0)
            if j + 2 < len(jobs):
                pend.append(issue_load(j + 2))
            bt, bB = br.next()
            S_.op(engs[j % 3], CPY(bt[0:P, 0:n], f[0:P, 0:n]), reads=[fB], writes=[bB])
            if len(dst.shape) == 3:
                S_.dma("sp", DMA(dst, bt[0:P, 0:n].rearrange("p (k c) -> p k c", c=dst.shape[2])), reads=[bB], writes=[Buf()])
            else:
                S_.dma("sp", DMA(dst, bt[0:P, 0:n]), reads=[bB], writes=[Buf()])
        S_.barrier()
        S_.emit()


def x_to_hT(g, S_, xs, xsB, gbc, hb, hbB, junk, junkB, ss, ssB, identb, pbs, hT, hTB, col0, n_rms=D):
    S_.op("act", ACTF(junk, xs, AF.Square, accum_out=ss), reads=[xsB], writes=[junkB, ssB])
    rstd_ops(S_, ss, n_rms, EPS, ssB)
    S_.op("dve", STT(hb, xs, ss, gbc, ALU.mult, ALU.mult), reads=[xsB, ssB], writes=[hbB])
    for half in range(2):
        pb, pbB = pbs.next()
        for k in range(8):
            kk = half * 8 + k
            S_.op("pe", TR(pb[:, k * 128:(k + 1) * 128], hb[:, kk * 128:(kk + 1) * 128], identb), reads=[hbB], writes=[pbB])
        S_.op("act" if half == 0 else "dve",
              CPY(hT[:, half * 8:(half + 1) * 8, col0:col0 + 128], pb[:].rearrange("p (k t) -> p k t", t=128)),
              reads=[pbB], writes=[hTB])


def phase_a1(g):
    nc, S_ = g.nc, g.sch
    with contextlib.ExitStack() as s1:
        def sb(n, sh, dt):
            return s1.enter_context(nc.sbuf_tensor("a1_" + n, sh, dt))

        def ps(n, sh, dt):
            return s1.enter_context(nc.psum_tensor("a1_" + n, sh, dt))
        xr = Ring([(sb("xs%d" % i, [128, D], F32), Buf()) for i in range(3)])
        gbc = sb("gbc", [128, D], F32)
        hbr = Ring([(sb("hb%d" % i, [128, D], BF16), Buf()) for i in range(2)])
        junk = sb("junk", [128, D], BF16); junkB = Buf()
        hTr = Ring([(sb("hT%d" % i, [128, 16, TS], BF16), Buf()) for i in range(2)])
        wr = Ring([(sb("wblk%d" % i, [128, 16 * 512], BF16), Buf()) for i in range(2)])
        lat = sb("lat", [128, 4, 576], F32); latB = [Buf() for _ in range(4)]
        prr = Ring([(sb("prst%d" % i, [128, 512], F32), Buf()) for i in range(3)])
        ukv = sb("ukv", [128, 4, 2048], BF16)
        kvn = sb("kvn", [128, 512], BF16); kvnB = Buf()
        kvnT = sb("kvnT", [128, 4, 128], BF16); kvnTB = Buf()
        kvs = sb("kvs", [128, 2048], F32); kvsB = Buf()
        sq = sb("sq", [128, 1024], F32); sqB = Buf()
        kn = sb("kn", [128, 8, 128], BF16); knB = Buf()
        KTr = Ring([(sb("KTst%d" % i, [128, 8, TS], BF16), Buf()) for i in range(2)])
        Vr = Ring([(sb("Vst%d" % i, [128, 4, 1024], BF16), Buf()) for i in range(2)])
        KPr = Ring([(sb("KPst%d" % i, [64, TS], BF16), Buf()) for i in range(2)])
        csr = Ring([(sb("cs%d" % i, [128, 4, 64], F32), Buf()) for i in range(2)])
        gkv = sb("gkv", [128, 512], F32)
        gkn = sb("gkn", [128, 128], F32)
        gkr = sb("gkr", [128, 64], F32)
        identf = sb("identf", [128, 128], F32)
        identb = sb("identb", [128, 128], BF16)
        ssr = Ring([(sb("ss%d" % i, [128, 1], F32), Buf()) for i in range(4)])
        ssk = sb("ssk", [128, 8], F32); sskB = Buf()
        krn = sb("krn", [128, 64], F32); krnB = Buf()
        kr2 = sb("kr2", [128, 64], F32); kr2B = Buf()
        kpe = sb("kpe", [128, 64], BF16); kpeB = Buf()
        pfs = Ring([(ps("pf%d" % i, [128, 512], F32), Buf()) for i in range(6)])
        pbs = Ring([(ps("pb%d" % i, [128, 1024], BF16), Buf()) for i in range(2)])
        cB = Buf()
        S_.dma("sp", DMA(gbc[:], g.attn_norm_g[0:1, :].partition_broadcast(128)), writes=[cB])
        S_.dma("sp", DMA(gkv[:], g.kv_lat_norm[0:1, :].partition_broadcast(128)), writes=[cB])
        S_.dma("sp", DMA(gkn[:], g.k_nope_norm[0:1, :].partition_broadcast(128)), writes=[cB])
        S_.dma("sp", DMA(gkr[:], g.k_rope_norm[0:1, :].partition_broadcast(128)), writes=[cB])
        S_.dma("sp", DMA(identf[:], g.ident[:, :]), writes=[cB])
        S_.op("dve", CPY(identb[:], identf[:]), reads=[cB], writes=[cB])
        S_.dma("sp", DMA(ukv[:].rearrange("p k c -> p (k c)"), g.ukvb[:, :]), writes=[cB])
        S_.barrier()

        def load_x(T, s):
            xs, xsB = xr.next()
            t0 = T * TS + s * 128
            S_.dma("sp", DMA(xs[:], g.x_true[t0:t0 + 128, :]), writes=[xsB])
            return xs, xsB

        def load_w(b):
            w, wB = wr.next()
            cb_ = g.ablocks[b][1]
            S_.dma("sp", DMA(w[:, 0:16 * cb_], g.wbA[b, :, 0:16 * cb_]), writes=[wB])
            return w, wB

        pend = [load_x(0, 0), load_x(0, 1)]
        wpend = [load_w(0)]
        for T in range(g.NT):
            hT, hTB = hTr.next()
            cs, csB = csr.next()
            S_.dma("sp", DMA(cs[:], g.cs_true[T * TS:(T + 1) * TS, :].rearrange("(s p) c -> p s c", p=128)), writes=[csB])
            for s in range(4):
                xs, xsB = pend.pop(0)
                nxt = T * 4 + s + 2
                if nxt < g.NT * 4:
                    pend.append(load_x(nxt // 4, nxt % 4))
                hb, hbB = hbr.next()
                ss, ssB = ssr.next()
                x_to_hT(g, S_, xs[:], xsB, gbc[:], hb[:], hbB, junk[:], junkB, ss[:], ssB, identb[:], pbs, hT, hTB, s * 128)
            KTst, KTB = KTr.next()
            Vst, VB = Vr.next()
            KPst, KPB = KPr.next()
            import os
            lvl = int(os.environ.get("A1STOP", "9"))
            if lvl < 2:
                continue
            for b, (c0, cb) in enumerate(g.ablocks):
                w, wB = wpend.pop(0)
                nb_ = (b + 1) % len(g.ablocks)
                if not (T == g.NT - 1 and b == len(g.ablocks) - 1):
                    wpend.append(load_w(nb_))
                wv = w[:, 0:16 * cb].rearrange("p (k c) -> p k c", c=cb)
                for s in range(4):
                    pf, pfB = pfs.next()
                    for k in range(16):
                        S_.op("pe", MM(pf[:, 0:cb], hT[:, k, s * 128:(s + 1) * 128], wv[:, k, :], k == 0, k == 15),
                              reads=[hTB, wB], writes=[pfB])
                    if b == 0:
                        S_.op("act", CPY(lat[:, s, 0:512], pf[:, :]), reads=[pfB], writes=[latB[s]])
                    elif b == 1:
                        S_.op("act", CPY(lat[:, s, 512:576], pf[:, 0:64]), reads=[pfB], writes=[latB[s]])
                    else:
                        pr, prB = prr.next()
                        S_.op("act" if (s % 2 == 0) else "dve", CPY(pr[:, 0:cb], pf[:, 0:cb]), reads=[pfB], writes=[prB])
                        t0 = T * TS + s * 128
                        S_.dma("sp", DMA(g.PR[t0:t0 + 128, c0 - 1344:c0 - 1344 + cb], pr[:, 0:cb]), reads=[prB], writes=[Buf()])
                if b >= 1 and lvl >= 3:
                    def seg1(s):
                        ss, ssB = ssr.next()
                        S_.op("act", ACTF(junk[:, 0:512], lat[:, s, 0:512], AF.Square, accum_out=ss[:]),
                              reads=[latB[s]], writes=[junkB, ssB])
                        rstd_ops(S_, ss[:], KVR, EPS, ssB)
                        S_.op("dve", STT(kvn[:], lat[:, s, 0:512], ss[:], gkv[:], ALU.mult, ALU.mult),
                              reads=[latB[s], ssB], writes=[kvnB])

                    def seg2(s):
                        pb, pbB = pbs.next()
                        for k in range(4):
                            S_.op("pe", TR(pb[:, k * 128:(k + 1) * 128], kvn[:, k * 128:(k + 1) * 128], identb[:]),
                                  reads=[kvnB], writes=[pbB])
                        S_.op("act", CPY(kvnT[:], pb[:, 0:512].rearrange("p (k t) -> p k t", t=128)), reads=[pbB], writes=[kvnTB])
                        for n in range(4):
                            pf, pfB = pfs.next()
                            for k in range(4):
                                S_.op("pe", MM(pf[:, :], kvnT[:, k, :], ukv[:, k, n * 512:(n + 1) * 512], k == 0, k == 3),
                                      reads=[kvnTB], writes=[pfB])
                            S_.op("act" if n % 2 == 0 else "dve", CPY(kvs[:, n * 512:(n + 1) * 512], pf[:, :]),
                                  reads=[pfB], writes=[kvsB])
                        kv3 = kvs[:].rearrange("p (h c) -> p h c", c=256)
                        S_.op("dve", CPY(Vst[:, s, :].rearrange("p (h c) -> p h c", c=128), kv3[:, :, 128:256]),
                              reads=[kvsB], writes=[VB])
                        sq3 = sq[:].rearrange("p (h c) -> p h c", c=128)
                        S_.op("dve", TT(sq3, kv3[:, :, 0:128], kv3[:, :, 0:128], ALU.mult), reads=[kvsB], writes=[sqB])
                        S_.op("dve", RED(ssk[:], sq3), reads=[sqB], writes=[sskB])
                        rstd_ops(S_, ssk[:], NOPE, EPS, sskB)
                        S_.op("dve", TT(sq3, kv3[:, :, 0:128], bc_last(ssk[:], 128), ALU.mult), reads=[kvsB, sskB], writes=[sqB])
                        S_.op("dve", TT(kn[:], sq3, bc_mid(gkn[:], 8), ALU.mult), reads=[sqB], writes=[knB])
                        ss, ssB = ssr.next()
                        S_.op("act", ACTF(junk[:, 0:64], lat[:, s, 512:576], AF.Square, accum_out=ss[:]),
                              reads=[latB[s]], writes=[junkB, ssB])
                        rstd_ops(S_, ss[:], ROPE, EPS, ssB)
                        S_.op("dve", STT(krn[:], lat[:, s, 512:576], ss[:], gkr[:], ALU.mult, ALU.mult),
                              reads=[latB[s], ssB], writes=[krnB])
                        rope_ops(S_, krn[:].rearrange("p (o c) -> p o c", o=1), krnB, kr2[:].rearrange("p (o c) -> p o c", o=1), kr2B,
                                 kpe[:].rearrange("p (o c) -> p o c", o=1), kpeB, cs[:, s, :], csB, 1)

                    def seg3(s):
                        pb, pbB = pbs.next()
                        for h in range(8):
                            S_.op("pe", TR(pb[:, h * 128:(h + 1) * 128], kn[:, h, :], identb[:]), reads=[knB], writes=[pbB])
                        S_.op("act", CPY(KTst[:, :, s * 128:(s + 1) * 128], pb[:].rearrange("p (h t) -> p h t", t=128)),
                              reads=[pbB], writes=[KTB])
                        pb, pbB = pbs.next()
                        S_.op("pe", TR(pb[0:64, 0:128], kpe[:], identb[:]), reads=[kpeB], writes=[pbB])
                        S_.op("act", CPY(KPst[:, s * 128:(s + 1) * 128], pb[0:64, 0:128]), reads=[pbB], writes=[KPB])
                    st_ = b - 1
                    if 0 <= st_ - 2 < 4:
                        seg3(st_ - 2)
                    if 0 <= st_ - 1 < 4:
                        seg2(st_ - 1)
                    if 0 <= st_ < 4:
                        seg1(st_)
                    if st_ == 5:
                        c0t = T * TS
                        S_.dma("sp", DMA(g.KT[:, :, c0t:c0t + TS].rearrange("h d t -> d h t"), KTst[:]), reads=[KTB], writes=[Buf()])
                        S_.dma("sp", DMA(g.V[c0t:c0t + TS, :].rearrange("(s p) f -> p s f", p=128), Vst[:]), reads=[VB], writes=[Buf()])
                        S_.dma("sp", DMA(g.KPT[:, c0t:c0t + TS], KPst[:]), reads=[KPB], writes=[Buf()])
        S_.barrier()
        S_.emit()


def rope_ops(S_, x, xB, tmp, tmpB, out, outB, cs, csB, nh):
    cos = bc_mid(cs[:, 0:32], nh)
    sin = bc_mid(cs[:, 32:64], nh)
    x1, x2 = x[:, :, 0:32], x[:, :, 32:64]
    t1, t2 = tmp[:, :, 0:32], tmp[:, :, 32:64]
    S_.op("dve", TT(t1, x2, sin, ALU.mult), reads=[xB, csB], writes=[tmpB])
    S_.op("dve", TT(t2, x1, sin, ALU.mult), reads=[xB, csB], writes=[tmpB])
    S_.op("dve", TT(x1, x1, cos, ALU.mult), reads=[xB, csB, tmpB], writes=[xB])
    S_.op("dve", TT(x2, x2, cos, ALU.mult), reads=[xB, csB, tmpB], writes=[xB])
    S_.op("dve", TT(out[:, :, 0:32], x1, t1, ALU.subtract), reads=[xB, tmpB], writes=[outB])
    S_.op("dve", TT(out[:, :, 32:64], x2, t2, ALU.add), reads=[xB, tmpB], writes=[outB])


def host_consts(S):
    c = {}
    c["ident"] = np.eye(128, dtype=np.float32)
    p = np.arange(128)[:, None, None]
    kb = np.arange(4)[None, :, None]
    q = np.arange(512)[None, None, :]
    c["diagmask"] = ((kb * 128 + p) <= q).astype(np.float32).reshape(128, 2048)
    s = np.arange(128)[:, None]
    t = np.arange(128)[None, :]
    c["rmask_ar"] = np.concatenate([(s < t), (s <= t)], axis=1).astype(np.float32)
    c["rmask_lo"] = (s > t).astype(np.float32)
    tri = np.concatenate([(s <= t), (s < t), (s > t), np.ones((128, 1), bool)], axis=1).astype(np.float32)
    c["tri"] = tri
    return c


def rope_table(pos):
    inv_freq = (1.0 / (10000.0 ** (np.arange(0, 64, 2, dtype=np.float32) / np.float32(64)))).astype(np.float32)
    ang = pos.astype(np.float32)[:, None] * inv_freq[None, :]
    return np.concatenate([np.cos(ang), np.sin(ang)], axis=1).astype(np.float32)


def make_in_maps(inputs, S, n_batch):
    consts = host_consts(S)
    maps = []
    names = ["attn_norm_g", "w_in", "q_lat_norm", "w_uq", "kv_lat_norm", "w_ukv", "q_head_norm", "k_nope_norm",
             "k_rope_norm", "rwkv_shift_mix", "rwkv_w0", "rwkv_w2", "rwkv_a0", "rwkv_a2", "rwkv_g2", "rwkv_k_k",
             "rwkv_k_a", "rwkv_r_k", "rwkv_ln_g", "rwkv_ln_b", "w_out", "ffn_norm_g", "w_gate", "w_up", "w_down"]
    shared = {}
    for n in names:
        a = np.asarray(inputs[n])[0]
        if a.ndim == 1:
            a = a[None, :]
        if n == "rwkv_r_k":
            a = a.reshape(1, RW)
        shared[n] = np.ascontiguousarray(a, dtype=np.float32)
    shared.update(consts)
    x = np.asarray(inputs["x"])
    NSL = S // TS // 2
    for b in range(n_batch):
        for c in range(2):
            m = dict(shared)
            own = own_tiles(S, c)
            m["x_true"] = np.ascontiguousarray(x[b])
            m["x_own"] = np.ascontiguousarray(np.concatenate([x[b, t * TS:(t + 1) * TS] for t in own], axis=0))
            pos_own = np.concatenate([np.arange(t * TS, (t + 1) * TS) for t in own])
            m["cs_true"] = rope_table(np.arange(S))
            m["cs_own"] = rope_table(pos_own)
            sfirst = np.array([1.0 if own[j] == 2 * j else 0.0 for j in range(NSL)], np.float32)
            m["sel"] = np.concatenate([sfirst, 1.0 - sfirst])[None, :].astype(np.float32)
            maps.append(m)
    return maps


_NC_CACHE = {}


def kernel(**inputs):
    x = np.asarray(inputs["x"])
    B, S, _ = x.shape
    if S not in _NC_CACHE:
        _NC_CACHE[S] = build_nc(S)
    nc = _NC_CACHE[S]
    maps = make_in_maps(inputs, S, B)
    res = run_bass_kernel_spmd(nc, maps, core_ids=list(range(2 * B)))
    out = np.empty((B, S, D), np.float32)
    for b in range(B):
        for c in range(2):
            o = res.results[2 * b + c]["out"]
            for j, t in enumerate(own_tiles(S, c)):
                out[b, t * TS:(t + 1) * TS] = o[j * TS:(j + 1) * TS]
    return out


def phase_a2(g):
    nc, S_ = g.nc, g.sch
    with contextlib.ExitStack() as s1:
        def sb(n, sh, dt):
            return s1.enter_context(nc.sbuf_tensor("a2_" + n, sh, dt))

        def ps(n, sh, dt):
            return s1.enter_context(nc.psum_tensor("a2_" + n, sh, dt))
        xr = Ring([(sb("xs%d" % i, [128, D], F32), Buf()) for i in range(3)])
        gbc = sb("gbc", [128, D], F32)
        hbr = Ring([(sb("hb%d" % i, [128, D], BF16), Buf()) for i in range(2)])
        junk = sb("junk", [128, D], BF16); junkB = Buf()
        hTr = Ring([(sb("hT%d" % i, [128, 16, 128], BF16), Buf()) for i in range(2)])
        wq = sb("wq", [128, 16, 768], BF16)
        uq = sb("uq", [128, 6, 1536], BF16)
        ql = sb("ql", [128, 768], F32); qlB = Buf()
        qlb = sb("qlb", [128, 768], BF16); qlbB = Buf()
        qlT = sb("qlT", [128, 6, 128], BF16); qlTB = Buf()
        qf = sb("qf", [128, 8, 192], F32); qfB = Buf()
        sqq = sb("sqq", [128, 8, 192], F32); sqqB = Buf()
        qtmp = sb("qtmp", [128, 8, 64], F32); qtmpB = Buf()
        qb = sb("qb", [128, 8, 192], BF16); qbB = Buf()
        QTr = Ring([(sb("QTst%d" % i, [128, 8, TS], BF16), Buf()) for i in range(2)])
        QPr = Ring([(sb("QPst%d" % i, [64, 8, TS], BF16), Buf()) for i in range(2)])
        csr = Ring([(sb("cs%d" % i, [128, 4, 64], F32), Buf()) for i in range(2)])
        gql = sb("gql", [128, 768], F32)
        gqh = sb("gqh", [128, 192], F32)
        identf = sb("identf", [128, 128], F32)
        identb = sb("identb", [128, 128], BF16)
        ssr = Ring([(sb("ss%d" % i, [128, 1], F32), Buf()) for i in range(4)])
        ssq = sb("ssq", [128, 8], F32); ssqB = Buf()
        pfs = Ring([(ps("pf%d" % i, [128, 512], F32), Buf()) for i in range(6)])
        pbs = Ring([(ps("pb%d" % i, [128, 1024], BF16), Buf()) for i in range(2)])
        cB = Buf()
        S_.dma("sp", DMA(gbc[:], g.attn_norm_g[0:1, :].partition_broadcast(128)), writes=[cB])
        S_.dma("sp", DMA(gql[:], g.q_lat_norm[0:1, :].partition_broadcast(128)), writes=[cB])
        S_.dma("sp", DMA(gqh[:], g.q_head_norm[0:1, :].partition_broadcast(128)), writes=[cB])
        S_.dma("sp", DMA(identf[:], g.ident[:, :]), writes=[cB])
        S_.dma("sp", DMA(wq[:].rearrange("p k c -> p (k c)"), g.wqb[:, :]), writes=[cB])
        S_.dma("sp", DMA(uq[:].rearrange("p k c -> p (k c)"), g.uqb[:, :]), writes=[cB])
        S_.op("dve", CPY(identb[:], identf[:]), reads=[cB], writes=[cB])
        S_.op("dve", TSC(gqh[:], gqh[:], float(QK ** -0.5), None, ALU.mult), reads=[cB], writes=[cB])
        S_.barrier()

        def load_x(n):
            xs, xsB = xr.next()
            S_.dma("sp", DMA(xs[:], g.x_own[n * 128:(n + 1) * 128, :]), writes=[xsB])
            return xs, xsB
        nsub = g.SO // 128
        pend = [load_x(0), load_x(1)]
        for n in range(nsub):
            T, s = n // 4, n % 4
            if s == 0:
                cs, csB = csr.next()
                S_.dma("sp", DMA(cs[:], g.cs_own[T * TS:(T + 1) * TS, :].rearrange("(s p) c -> p s c", p=128)), writes=[csB])
                QTst, QTB = QTr.next()
                QPst, QPB = QPr.next()
            xs, xsB = pend.pop(0)
            if n + 2 < nsub:
                pend.append(load_x(n + 2))
            hb, hbB = hbr.next()
            hT, hTB = hTr.next()
            ss, ssB = ssr.next()
            x_to_hT(g, S_, xs[:], xsB, gbc[:], hb[:], hbB, junk[:], junkB, ss[:], ssB, identb[:], pbs, hT, hTB, 0)
            for (c0, cb) in ((0, 512), (512, 256)):
                pf, pfB = pfs.next()
                for k in range(16):
                    S_.op("pe", MM(pf[:, 0:cb], hT[:, k, :], wq[:, k, c0:c0 + cb], k == 0, k == 15), reads=[hTB], writes=[pfB])
                S_.op("act", CPY(ql[:, c0:c0 + cb], pf[:, 0:cb]), reads=[pfB], writes=[qlB])
            ss, ssB = ssr.next()
            S_.op("act", ACTF(junk[:, 0:768], ql[:], AF.Square, accum_out=ss[:]), reads=[qlB], writes=[junkB, ssB])
            rstd_ops(S_, ss[:], QR, EPS, ssB)
            S_.op("dve", STT(qlb[:], ql[:], ss[:], gql[:], ALU.mult, ALU.mult), reads=[qlB, ssB], writes=[qlbB])
            pb, pbB = pbs.next()
            for k in range(6):
                S_.op("pe", TR(pb[:, k * 128:(k + 1) * 128], qlb[:, k * 128:(k + 1) * 128], identb[:]), reads=[qlbB], writes=[pbB])
            S_.op("act", CPY(qlT[:], pb[:, 0:768].rearrange("p (k t) -> p k t", t=128)), reads=[pbB], writes=[qlTB])
            for nb in range(4):
                pf, pfB = pfs.next()
                for k in range(6):
                    S_.op("pe", MM(pf[:, 0:384], qlT[:, k, :], uq[:, k, nb * 384:(nb + 1) * 384], k == 0, k == 5), reads=[qlTB], writes=[pfB])
                S_.op("act" if nb % 2 == 0 else "dve",
                      CPY(qf[:, 2 * nb:2 * nb + 2, :], pf[:, 0:384].rearrange("p (h c) -> p h c", c=192)), reads=[pfB], writes=[qfB])
            S_.op("dve", TT(sqq[:], qf[:], qf[:], ALU.mult), reads=[qfB], writes=[sqqB])
            S_.op("dve", RED(ssq[:], sqq[:]), reads=[sqqB], writes=[ssqB])
            rstd_ops(S_, ssq[:], QK, EPS, ssqB)
            S_.op("dve", TT(sqq[:], qf[:], bc_last(ssq[:], 192), ALU.mult), reads=[qfB, ssqB], writes=[sqqB])
            S_.op("dve", TT(qf[:], sqq[:], bc_mid(gqh[:], 8), ALU.mult), reads=[sqqB], writes=[qfB])
            S_.op("act", CPY(qb[:, :, 0:128], qf[:, :, 0:128]), reads=[qfB], writes=[qbB])
            rope_ops(S_, qf[:, :, 128:192], qfB, qtmp[:], qtmpB, qb[:, :, 128:192], qbB, cs[:, s, :], csB, 8)
            for half in range(2):
                pb, pbB = pbs.next()
                for h in range(8):
                    if half == 0:
                        S_.op("pe", TR(pb[:, h * 128:(h + 1) * 128], qb[:, h, 0:128], identb[:]), reads=[qbB], writes=[pbB])
                    else:
                        S_.op("pe", TR(pb[0:64, h * 128:(h + 1) * 128], qb[:, h, 128:192], identb[:]), reads=[qbB], writes=[pbB])
                if half == 0:
                    S_.op("act", CPY(QTst[:, :, s * 128:(s + 1) * 128], pb[:].rearrange("p (h t) -> p h t", t=128)), reads=[pbB], writes=[QTB])
                else:
                    S_.op("dve", CPY(QPst[:, :, s * 128:(s + 1) * 128], pb[0:64, :].rearrange("p (h t) -> p h t", t=128)), reads=[pbB], writes=[QPB])
            if s == 3:
                c0t = T * TS
                S_.dma("sp", DMA(g.QT[:, :, c0t:c0t + TS].rearrange("h d t -> d h t"), QTst[:]), reads=[QTB], writes=[Buf()])
                S_.dma("sp", DMA(g.QPT[:, :, c0t:c0t + TS].rearrange("h d t -> d h t"), QPst[:]), reads=[QPB], writes=[Buf()])
        S_.barrier()
        S_.emit()


def phase_c(g):
    nc, S_ = g.nc, g.sch
    S, SO, NSL = g.S, g.SO, g.NSL
    with contextlib.ExitStack() as s1:
        def sb(n, sh, dt):
            return s1.enter_context(nc.sbuf_tensor("c_" + n, sh, dt))

        def ps(n, sh, dt):
            return s1.enter_context(nc.psum_tensor("c_" + n, sh, dt))
        hr = Ring([((sb("KTh%d" % i, [128, S], BF16), sb("Vh%d" % i, [128, S // 128, 128], BF16),
                     sb("QTh%d" % i, [128, SO], BF16), sb("QPh%d" % i, [64, SO], BF16)), Buf()) for i in range(2)])
        KP = sb("KP", [64, S], BF16)
        dmf = sb("dmf", [128, 2048], F32)
        dm = sb("dm", [128, 2048], BF16)
        mA = sb("mA", [128, 2048], BF16); mAB = Buf()
        mB = sb("mB", [128, 2048], BF16); mBB = Buf()
        ones = sb("ones", [128, 128], BF16)
        selbc = sb("selbc", [128, 2 * NSL], F32)
        ptr = Ring([(sb("pt%d" % i, [128, 512], BF16), Buf()) for i in range(4)])
        recr = Ring([(sb("rec%d" % i, [128, 512], F32), Buf()) for i in range(2)])
        aor = Ring([(sb("ao%d" % i, [128, 512], BF16), Buf()) for i in range(2)])
        stp = Ring([(ps("st%d" % i, [128, 512], F32), Buf()) for i in range(3)])
        opr = Ring([(ps("op%d" % i, [128, 512], F32), Buf()) for i in range(2)])
        dnr = Ring([(ps("dn%d" % i, [128, 512], F32), Buf()) for i in range(2)])
        cB = Buf()
        S_.dma("sp", DMA(KP[:], g.KPT[:, :]), writes=[cB])
        S_.dma("sp", DMA(dmf[:], g.diagmask[:, :]), writes=[cB])
        S_.dma("sp", DMA(selbc[:], g.sel[0:1, :].partition_broadcast(128)), writes=[cB])
        S_.op("dve", CPY(dm[:], dmf[:]), reads=[cB], writes=[cB])
        S_.op("dve", MEMSET(ones[:], 1.0), writes=[cB])
        S_.barrier()

        def load_head(h):
            (KTh, Vh, QTh, QPh), hB = hr.next()
            S_.dma("sp", DMA(KTh[:], g.KT[h]), writes=[hB])
            S_.dma("sp", DMA(Vh[:], g.V[:, h * 128:(h + 1) * 128].rearrange("(n p) d -> p n d", p=128)), writes=[hB])
            S_.dma("sp", DMA(QTh[:], g.QT[h]), writes=[hB])
            S_.dma("sp", DMA(QPh[:], g.QPT[h]), writes=[hB])
            return (KTh, Vh, QTh, QPh), hB
        nxt = load_head(0)
        for h in range(NH):
            (KTh, Vh, QTh, QPh), hB = nxt
            if h + 1 < NH:
                nxt = load_head(h + 1)
            for j in range(NSL):
                nkb = 4 * (2 * j + 2)
                qc = slice(j * TS, (j + 1) * TS)
                S_.op("pool", TSC(mA[:], dm[:], selbc[:, j:j + 1], selbc[:, NSL + j:NSL + j + 1], ALU.mult, ALU.add), writes=[mAB])
                S_.op("pool", TSC(mB[:], dm[:], selbc[:, NSL + j:NSL + j + 1], None, ALU.mult), writes=[mBB])
                O, OB = opr.next()
                dn, dnB = dnr.next()
                sts = {}
                pts = {}

                def qk(kb):
                    st, stB = stp.next()
                    kc = slice(kb * 128, (kb + 1) * 128)
                    S_.op("pe", MM(st[:, :], KTh[:, kc], QTh[:, qc], True, False), reads=[hB], writes=[stB])
                    S_.op("pe", MM(st[:, :], KP[:, kc], QPh[:, qc], False, True), reads=[hB], writes=[stB])
                    sts[kb] = (st, stB)

                def ex(kb):
                    st, stB = sts.pop(kb)
                    pt, ptB = ptr.next()
                    S_.op("act", ACTF(pt[:], st[:, :], AF.Exp), reads=[stB], writes=[ptB])
                    u = kb // 4
                    if u == 2 * j:
                        S_.op("dve", TT(pt[:], pt[:], mA[:, (kb % 4) * 512:(kb % 4 + 1) * 512], ALU.mult), reads=[ptB, mAB], writes=[ptB])
                    elif u == 2 * j + 1:
                        S_.op("dve", TT(pt[:], pt[:], mB[:, (kb % 4) * 512:(kb % 4 + 1) * 512], ALU.mult), reads=[ptB, mBB], writes=[ptB])
                    pts[kb] = (pt, ptB)

                def pv(kb):
                    pt, ptB = pts.pop(kb)
                    S_.op("pe", MM(O[:, :], Vh[:, kb, :], pt[:], kb == 0, kb == nkb - 1), reads=[hB, ptB], writes=[OB])
                    S_.op("pe", MM(dn[:, :], ones[:], pt[:], kb == 0, kb == nkb - 1), reads=[ptB], writes=[dnB])
                qk(0)
                qk(1)
                for kb in range(nkb):
                    ex(kb)
                    if kb + 2 < nkb:
                        qk(kb + 2)
                    pv(kb)
                rec, recB = recr.next()
                ao, aoB = aor.next()
                S_.op("dve", RCP(rec[:], dn[:, :]), reads=[dnB], writes=[recB])
                S_.op("dve", TT(ao[:], O[:, :], rec[:], ALU.mult), reads=[OB, recB], writes=[aoB])
                S_.dma("sp", DMA(g.AT[h * 128:(h + 1) * 128, qc], ao[:]), reads=[aoB], writes=[Buf()])
        S_.barrier()
        S_.emit()


def phase_e(g):
    nc, S_ = g.nc, g.sch
    NSL = g.NSL
    with contextlib.ExitStack() as s1:
        def sb(n, sh, dt):
            return s1.enter_context(nc.sbuf_tensor("e_" + n, sh, dt))

        def ps(n, sh, dt):
            return s1.enter_context(nc.psum_tensor("e_" + n, sh, dt))
        actT = sb("actT", [128, 16, TS], BF16); actB = Buf()
        btmp = sb("btmp", [128, 8, TS], BF16); btB = Buf()
        x1 = sb("x1", [128, 4, D], F32); x1B = [Buf() for _ in range(4)]
        hidT = sb("hidT", [128, 44, TS], BF16); hidB = Buf()
        wor = Ring([(sb("wo%d" % i, [128, 16, 256], BF16), Buf()) for i in range(2)])
        wgr = Ring([(sb("wgu%d" % i, [128, 2, 16, 128], BF16), Buf()) for i in range(4)])
        wdr = Ring([(sb("wd%d" % i, [128, 11, 512], BF16), Buf()) for i in range(3)])
        obr = Ring([(sb("ob%d" % i, [128, 512], F32), Buf()) for i in range(3)])
        sgr = Ring([(sb("sg%d" % i, [128, 512], BF16), Buf()) for i in range(2)])
        gff = sb("gff", [128, D], F32)
        hbr = Ring([(sb("hb%d" % i, [128, D], BF16), Buf()) for i in range(2)])
        identf = sb("identf", [128, 128], F32)
        identb = sb("identb", [128, 128], BF16)
        selbc = sb("selbc", [128, 2 * NSL], F32)
        ssr = Ring([(sb("ss%d" % i, [128, 1], F32), Buf()) for i in range(4)])
        pfs = Ring([(ps("pf%d" % i, [128, 512], F32), Buf()) for i in range(6)])
        pbs = Ring([(ps("pb%d" % i, [128, 1024], BF16), Buf()) for i in range(2)])
        cB = Buf()
        S_.dma("sp", DMA(gff[:], g.ffn_norm_g[0:1, :].partition_broadcast(128)), writes=[cB])
        S_.dma("sp", DMA(identf[:], g.ident[:, :]), writes=[cB])
        S_.dma("sp", DMA(selbc[:], g.sel[0:1, :].partition_broadcast(128)), writes=[cB])
        S_.op("dve", CPY(identb[:], identf[:]), reads=[cB], writes=[cB])
        S_.barrier()
        def mk_wo(b):
            def f():
                w, wB = wor.next()
                S_.dma("sp", DMA(w[:].rearrange("p k c -> p (k c)"), g.wbO[b]), writes=[wB])
                return w, wB
            return f

        def mk_wg(hc):
            def f():
                w, wB = wgr.next()
                S_.dma("sp", DMA(w[:].rearrange("p a k c -> p (a k c)"), g.wbGU[hc]), writes=[wB])
                return w, wB
            return f

        def mk_wd(i):
            def f():
                w, wB = wdr.next()
                S_.dma("sp", DMA(w[:].rearrange("p k c -> p (k c)"), g.wbD[i]), writes=[wB])
                return w, wB
            return f
        for j in range(NSL):
            wl = [mk_wo(b) for b in range(8)] + [mk_wg(hc) for hc in range(44)] + [mk_wd(i) for i in range(16)]
            issued = {}
            nissued = [0]

            def getw(i):
                for ii in range(i, i + 4):
                    depth = 3 if 8 <= ii < 52 else (2 if ii >= 52 else 1)
                    if ii >= len(wl):
                        break
                    if ii < nissued[0]:
                        continue
                    if (ii - i) > depth:
                        break
                    issued[ii] = wl[ii]()
                    nissued[0] = ii + 1
                return issued.pop(i)
            qc = slice(j * TS, (j + 1) * TS)
            S_.dma("sp", DMA(actT[:, 0:8, :], g.AT[:, qc].rearrange("(k p) t -> p k t", p=128)), writes=[actB])
            S_.dma("sp", DMA(actT[:, 8:16, :], g.BT[:, (2 * j) * TS:(2 * j + 1) * TS].rearrange("(k p) t -> p k t", p=128)), writes=[actB])
            S_.dma("sp", DMA(btmp[:], g.BT[:, (2 * j + 1) * TS:(2 * j + 2) * TS].rearrange("(k p) t -> p k t", p=128)), writes=[btB])
            for s in range(4):
                S_.dma("sp", DMA(x1[:, s, :], g.x_own[j * TS + s * 128:j * TS + (s + 1) * 128, :]), writes=[x1B[s]])
            S_.op("pool", TSC(btmp[:], btmp[:], selbc[:, NSL + j:NSL + j + 1], None, ALU.mult), reads=[btB], writes=[btB])
            S_.op("dve", STT(actT[:, 8:16, :], actT[:, 8:16, :], selbc[:, j:j + 1], btmp[:], ALU.mult, ALU.add), reads=[actB, btB], writes=[actB])
            for ob in range(8):
                w, wB = getw(ob)
                for s in range(4):
                    pf, pfB = pfs.next()
                    for k in range(16):
                        S_.op("pe", MM(pf[:, 0:256], actT[:, k, s * 128:(s + 1) * 128], w[:, k, :], k == 0, k == 15), reads=[actB, wB], writes=[pfB])
                    xs = x1[:, s, ob * 256:(ob + 1) * 256]
                    S_.op("dve", TT(xs, xs, pf[:, 0:256], ALU.add), reads=[pfB, x1B[s]], writes=[x1B[s]])
            for s in range(4):
                hb, hbB = hbr.next()
                ss, ssB = ssr.next()
                x_to_hT(g, S_, x1[:, s, :], x1B[s], gff[:], hb[:], hbB, hb[:], hbB, ss[:], ssB, identb[:], pbs, actT, actB, s * 128)
            for hc in range(44):
                w, wB = getw(8 + hc)
                pg, pgB = pfs.next()
                pu, puB = pfs.next()
                for k in range(16):
                    S_.op("pe", MM(pg[:, :], w[:, 0, k, :], actT[:, k, :], k == 0, k == 15), reads=[actB, wB], writes=[pgB])
                for k in range(16):
                    S_.op("pe", MM(pu[:, :], w[:, 1, k, :], actT[:, k, :], k == 0, k == 15), reads=[actB, wB], writes=[puB])
                sg, sgB = sgr.next()
                S_.op("act", ACTF(sg[:], pg[:, :], AF.Silu), reads=[pgB], writes=[sgB])
                S_.op("dve", TT(hidT[:, hc, :], pu[:, :], sg[:], ALU.mult), reads=[puB, sgB], writes=[hidB])
            for fb in range(4):
                banks = [pfs.next() for _ in range(4)]
                for hg in range(4):
                    w, wB = getw(52 + fb * 4 + hg)
                    for s in range(4):
                        pf, pfB = banks[s]
                        for k in range(11):
                            S_.op("pe", MM(pf[:, :], hidT[:, hg * 11 + k, s * 128:(s + 1) * 128], w[:, k, :],
                                           hg == 0 and k == 0, hg == 3 and k == 10), reads=[hidB, wB], writes=[pfB])
                for s in range(4):
                    pf, pfB = banks[s]
                    o, oB = obr.next()
                    S_.op("dve", TT(o[:], pf[:, :], x1[:, s, fb * 512:(fb + 1) * 512], ALU.add), reads=[pfB, x1B[s]], writes=[oB])
                    r0 = j * TS + s * 128
                    S_.dma("sp", DMA(g.out[r0:r0 + 128, fb * 512:(fb + 1) * 512], o[:]), reads=[oB], writes=[Buf()])
        S_.barrier()
        S_.emit()


def phase_d(g):
    nc, S_ = g.nc, g.sch
    NCH = g.NCH
    with contextlib.ExitStack() as s1:
        def sb(n, sh, dt):
            return s1.enter_context(nc.sbuf_tensor("d_" + n, sh, dt))

        def ps(n, sh, dt):
            return s1.enter_context(nc.psum_tensor("d_" + n, sh, dt))
        mixbc = sb("mixbc", [128, RIN], F32)
        cst = {}
        for nm, src in (("kk", g.k_k), ("ka", g.k_a), ("rk", g.r_k), ("w0", g.w0), ("a0", g.a0), ("lng", g.ln_g), ("lnb", g.ln_b)):
            cst[nm] = sb("c_" + nm, [128, RW], F32)
        w2b = sb("w2b", [64, RW], BF16); a2b = sb("a2b", [64, RW], BF16)
        g2a = sb("g2a", [128, RW], BF16); g2c = sb("g2c", [32, RW], BF16)
        tri = sb("tri", [128, 385], F32)
        trib = sb("trib", [128, 385], BF16)
        onesb = sb("onesb", [128, 16], BF16)
        sgh = sb("sgh", [128, RW], BF16); sghB = Buf()
        sgm = sb("sgm", [128, RW], BF16); sgmB = Buf()
        mar = sb("mar", [128, 256], F32); mlo = sb("mlo", [128, 128], F32)
        identf = sb("identf", [128, 128], F32); identb = sb("identb", [128, 128], BF16)
        curr = Ring([(sb("cur%d" % i, [128, RIN], F32), Buf()) for i in range(2)])
        prv = sb("prv", [128, RIN], F32); prvB = Buf()
        F = [sb("F%d" % i, [128, RW], F32) for i in range(7)]
        FB = [Buf() for _ in range(7)]
        lin = sb("lin", [128, 288], BF16); linB = Buf()
        linT = sb("linT", [128, 512], BF16); linTB = Buf()
        gt = sb("gt", [128, RW], BF16); gtB = Buf()
        tk = {}
        tkB = {}
        for nm in ("rt", "at", "bt", "kt", "bh", "kh", "vb"):
            tk[nm] = sb("t_" + nm, [128, RW], BF16); tkB[nm] = Buf()
        AR = sb("AR", [128, 8, 2, 2, 128], BF16); ARB = Buf()
        BTf = sb("BTf", [128, 8, 128], BF16); BTfB = Buf()
        KTf = sb("KTf", [128, 8, 128], BF16); KTfB = Buf()
        MakT = sb("MakT", [128, 16, 128], BF16); MakB = Buf()
        MrbT = sb("MrbT", [128, 16, 128], BF16); MrbB = Buf()
        MrkT = sb("MrkT", [128, 16, 128], BF16); MrkB = Buf()
        G7 = sb("G7", [128, 16, 128], BF16); G7B = Buf()
        Nm = sb("Nm", [128, 8, 128], BF16); NmB = [Buf() for _ in range(4)]
        NT = sb("NT", [128, 8, 128], BF16); NTB = [Buf() for _ in range(4)]
        GQ = [sb("GQ%d" % i, [128, 8, 2, 128], BF16) for i in range(2)]; GQB = [[Buf() for _ in range(4)] for _ in range(2)]
        QTt = [sb("QTt%d" % i, [128, 8, 128], BF16) for i in range(2)]; QTB = [[Buf() for _ in range(2)] for _ in range(2)]
        W1b = sb("W1b", [128, RW], BF16); W1B = [Buf(), Buf()]
        Ub = sb("Ub", [128, RW], BF16); UbB = [Buf(), Buf()]
        H = sb("H", [128, 8, 64], F32); HB = Buf()
        Hb = sb("Hb", [128, 8, 64], BF16); HbB = Buf()
        PC = sb("PC", [128, 8], F32); PCB = Buf()
        ssk = sb("ssk", [128, 16], F32); sskB = Buf()
        bon = sb("bon", [128, 16], F32); bonB = Buf()
        mu = sb("mu", [128, 16], F32); muB = Buf()
        var = sb("var", [128, 16], F32); varB = Buf()
        obf = sb("obf", [128, RW], BF16); obfB = Buf()
        BTr = Ring([(sb("BTs%d" % i, [128, 8, 128], BF16), Buf()) for i in range(1)])
        pfs = Ring([(ps("pf%d" % i, [128, 512], F32), Buf()) for i in range(6)])
        pbs = Ring([(ps("pb%d" % i, [128, 1024], BF16), Buf()) for i in range(2)])
        cB = Buf()
        S_.dma("sp", DMA(mixbc[:], g.shift_mix[0:1, :].partition_broadcast(128)), writes=[cB])
        for nm, src in (("kk", g.k_k), ("ka", g.k_a), ("rk", g.r_k), ("w0", g.w0), ("a0", g.a0), ("lng", g.ln_g), ("lnb", g.ln_b)):
            S_.dma("sp", DMA(cst[nm][:], src[0:1, :].partition_broadcast(128)), writes=[cB])
        S_.dma("sp", DMA(w2b[:], g.w2b[:, :]), writes=[cB])
        S_.dma("sp", DMA(a2b[:], g.a2b[:, :]), writes=[cB])
        S_.dma("sp", DMA(g2a[:], g.g2b[0:128, :]), writes=[cB])
        S_.dma("sp", DMA(g2c[:], g.g2b[128:160, :]), writes=[cB])
        S_.dma("sp", DMA(tri[:], g.tri[:, :]), writes=[cB])
        S_.dma("sp", DMA(mar[:], g.rmask_ar[:, :]), writes=[cB])
        S_.dma("sp", DMA(mlo[:], g.rmask_lo[:, :]), writes=[cB])
        S_.dma("sp", DMA(identf[:], g.ident[:, :]), writes=[cB])
        S_.op("dve", CPY(identb[:], identf[:]), reads=[cB], writes=[cB])
        S_.op("dve", CPY(trib[:], tri[:]), reads=[cB], writes=[cB])
        S_.op("dve", MEMSET(onesb[:], 1.0), writes=[cB])
        S_.op("dve", MEMSET(AR[:], 0.0), writes=[ARB])
        S_.op("dve", MEMSET(H[:], 0.0), writes=[HB])
        S_.op("dve", MEMSET(Hb[:], 0.0), writes=[HbB])
        S_.barrier()
        h3 = lambda ap: ap.rearrange("p (h c) -> p h c", c=64)

        def load_cur(c):
            cur, curB = curr.next()
            S_.dma("sp", DMA(cur[:], g.PR[c * 128:(c + 1) * 128, :]), writes=[curB])
            return cur, curB

        def load_prv(c):
            if c == 0:
                S_.op("dve", MEMSET(prv[0:1, :], 0.0), writes=[prvB])
                S_.dma("sp", DMA(prv[1:128, :], g.PR[0:127, :]), writes=[prvB])
            else:
                S_.dma("sp", DMA(prv[:], g.PR[c * 128 - 1:c * 128 + 127, :]), writes=[prvB])
        import os
        gtr = Ring([(gt, gtB), (sb("gt1", [128, RW], BF16), Buf())])
        Y8 = sb("Y8", [128, RW], F32); Y8B = Buf()
        NCHL = int(os.environ.get('DCH', str(NCH)))

        class Cx:
            pass

        def st12(cx):
            cur, curB = cx.cur, cx.curB
            if True:
                S_.op("dve", TT(prv[:], prv[:], cur[:], ALU.subtract), reads=[prvB, curB], writes=[prvB])
                yield
                S_.op("pool", TT(prv[:], prv[:], mixbc[:], ALU.mult), reads=[prvB], writes=[prvB])
                yield
                S_.op("dve", TT(cur[:], cur[:], prv[:], ALU.add), reads=[prvB, curB], writes=[curB])
                yield
                if cx.c + 1 < NCHL:
                    load_prv(cx.c + 1)
                cx.r_, cx.k_, cx.v_ = cur[:, 0:1024], cur[:, 1024:2048], cur[:, 2048:3072]
                gt, gtB = gtr.next()
                cx.gt, cx.gtB = gt, gtB
                S_.op("act", ACTF(lin[:, 0:64], cur[:, 3072:3136], AF.Tanh), reads=[curB], writes=[linB])
                yield
                S_.op("act", ACTF(lin[:, 128:288], cur[:, 3200:3360], AF.Sigmoid), reads=[curB], writes=[linB])
                yield
                S_.op("dve", CPY(lin[:, 64:128], cur[:, 3136:3200]), reads=[curB], writes=[linB])
                yield
                pb, pbB = pbs.next()
                S_.op("pe", TR(pb[0:64, 0:128], lin[:, 0:64], identb[:]), reads=[linB], writes=[pbB])
                yield
                S_.op("pe", TR(pb[0:64, 128:256], lin[:, 64:128], identb[:]), reads=[linB], writes=[pbB])
                yield
                S_.op("pe", TR(pb[:, 256:384], lin[:, 128:256], identb[:]), reads=[linB], writes=[pbB])
                yield
                S_.op("pe", TR(pb[0:32, 384:512], lin[:, 256:288], identb[:]), reads=[linB], writes=[pbB])
                yield
                S_.op("act", CPY(linT[0:64, 0:256], pb[0:64, 0:256]), reads=[pbB], writes=[linTB])
                yield
                S_.op("act", CPY(linT[:, 256:384], pb[:, 256:384]), reads=[pbB], writes=[linTB])
                yield
                S_.op("act", CPY(linT[0:32, 384:512], pb[0:32, 384:512]), reads=[pbB], writes=[linTB])
                yield
                for n in range(2):
                    cs_ = slice(n * 512, (n + 1) * 512)
                    pf, pfB = pfs.next()
                    S_.op("pe", MM(pf[:, :], linT[0:64, 0:128], w2b[:, cs_]), reads=[linTB], writes=[pfB])
                    S_.op("dve", TT(F[0][:, cs_], pf[:, :], cst["w0"][:, cs_], ALU.add), reads=[pfB], writes=[FB[0]])
                    pf, pfB = pfs.next()
                    S_.op("pe", MM(pf[:, :], linT[0:64, 128:256], a2b[:, cs_]), reads=[linTB], writes=[pfB])
                    S_.op("dve", TT(F[1][:, cs_], pf[:, :], cst["a0"][:, cs_], ALU.add), reads=[pfB], writes=[FB[1]])
                    pf, pfB = pfs.next()
                    S_.op("pe", MM(pf[:, :], linT[:, 256:384], g2a[:, cs_], True, False), reads=[linTB], writes=[pfB])
                    S_.op("pe", MM(pf[:, :], linT[0:32, 384:512], g2c[:, cs_], False, True), reads=[linTB], writes=[pfB])
                    S_.op("act", CPY(gt[:, cs_], pf[:, :]), reads=[pfB], writes=[gtB])
                S_.op("act", ACTF(F[0][:], F[0][:], AF.Sigmoid), reads=[FB[0]], writes=[FB[0]])
                yield
                S_.op("act", ACTF(F[1][:], F[1][:], AF.Sigmoid), reads=[FB[1]], writes=[FB[1]])
                yield

            yield

        def st78(cx):
            if True:
                def hsl(h):
                    return slice(h * 64, (h + 1) * 64)

                def rows_of(h):
                    return slice((h % 2) * 64, (h % 2 + 1) * 64)
                w1p = [pfs.next(), pfs.next()]
                for h in range(16):
                    pf, pfB = w1p[h // 8]
                    o_ = slice((h % 8) * 64, (h % 8 + 1) * 64)
                    S_.op("pe", MM(pf[:, o_], AR[:, h // 2, h % 2, 0, :], Hb[:, h // 2, :], True, False), reads=[ARB, HbB], writes=[pfB])
                    S_.op("pe", MM(pf[:, o_], MakT[:, h, :], tk["vb"][:, hsl(h)], False, True), reads=[MakB, tkB["vb"]], writes=[pfB])
                S_.op("act", CPY(W1b[:, 0:512], w1p[0][0][:, :]), reads=[w1p[0][1]], writes=[W1B[0]])
                yield
                S_.op("dve", CPY(W1b[:, 512:1024], w1p[1][0][:, :]), reads=[w1p[1][1]], writes=[W1B[1]])
                yield
                up_ = [pfs.next(), pfs.next()]
                for h in range(16):
                    pf, pfB = up_[h // 8]
                    o_ = slice((h % 8) * 64, (h % 8 + 1) * 64)
                    S_.op("pe", MM(pf[:, o_], G7[:, h, :], W1b[:, hsl(h)]), reads=[G7B, W1B[h // 8]], writes=[pfB])
                S_.op("act", CPY(Ub[:, 0:512], up_[0][0][:, :]), reads=[up_[0][1]], writes=[UbB[0]])
                yield
                S_.op("dve", CPY(Ub[:, 512:1024], up_[1][0][:, :]), reads=[up_[1][1]], writes=[UbB[1]])
                yield
                yp = [pfs.next(), pfs.next()]
                for h in range(16):
                    pf, pfB = yp[h // 8]
                    o_ = slice((h % 8) * 64, (h % 8 + 1) * 64)
                    S_.op("pe", MM(pf[:, o_], AR[:, h // 2, h % 2, 1, :], Hb[:, h // 2, :], True, False), reads=[ARB, HbB], writes=[pfB])
                    S_.op("pe", MM(pf[:, o_], MrbT[:, h, :], Ub[:, hsl(h)], False, False), reads=[MrbB, UbB[h // 8]], writes=[pfB])
                    S_.op("pe", MM(pf[:, o_], MrkT[:, h, :], tk["vb"][:, hsl(h)], False, True), reads=[MrkB, tkB["vb"]], writes=[pfB])
                hn = [pfs.next(), pfs.next()]
                for h in range(16):
                    pf, pfB = hn[h // 8]
                    o_ = slice((h % 8) * 64, (h % 8 + 1) * 64)
                    pc_ = slice((h // 2) * 128, (h // 2 + 1) * 128)
                    S_.op("pe", MM(pf[:, o_], tk["bh"][:, pc_], Ub[:, hsl(h)], True, False), reads=[tkB["bh"], UbB[h // 8]], writes=[pfB])
                    S_.op("pe", MM(pf[:, o_], tk["kh"][:, pc_], tk["vb"][:, hsl(h)], False, True), reads=[tkB["kh"], tkB["vb"]], writes=[pfB])
                S_.op("act", CPY(Y8[:, 0:512], yp[0][0][:, :]), reads=[yp[0][1], Y8B], writes=[Y8B])
                yield
                S_.op("act", CPY(Y8[:, 512:1024], yp[1][0][:, :]), reads=[yp[1][1], Y8B], writes=[Y8B])
                yield
                for half in range(2):
                    rows = slice(half * 64, (half + 1) * 64)
                    S_.op("dve", TT(H[rows, :, :], H[rows, :, :], bc_last(PC[rows, :], 64), ALU.mult), reads=[HB, PCB, HbB], writes=[HB])
                    for q in range(2):
                        src = hn[q][0][rows, :].rearrange("p (a b c) -> p a b c", b=2, c=64)[:, :, half, :]
                        S_.op("dve", TT(H[rows, 4 * q:4 * q + 4, :], H[rows, 4 * q:4 * q + 4, :], src, ALU.add), reads=[HB, hn[q][1]], writes=[HB])
                S_.op("act", CPY(Hb[:], H[:]), reads=[HB], writes=[HbB])
                yield
                y3 = h3(Y8[:])
                S_.op("dve", RED(mu[:], y3), reads=[Y8B], writes=[muB])
                yield
                S_.op("dve", TSC(mu[:], mu[:], 1.0 / 64, None, ALU.mult), reads=[muB], writes=[muB])
                yield
                S_.op("dve", TT(h3(F[5][:]), y3, bc_last(mu[:], 64), ALU.subtract), reads=[Y8B, muB, FB[5]], writes=[FB[5]])
                yield
                S_.op("pool", TT(F[6][:], F[5][:], F[5][:], ALU.mult), reads=[FB[5], FB[6]], writes=[FB[6]])
                yield
                S_.op("dve", RED(var[:], h3(F[6][:])), reads=[FB[6]], writes=[varB])
                yield
                rstd_ops(S_, var[:], 64, GN_EPS, varB)
                yield
                S_.op("dve", TT(h3(F[5][:]), h3(F[5][:]), bc_last(var[:], 64), ALU.mult), reads=[FB[5], varB], writes=[FB[5]])
                yield
                S_.op("pool", TT(F[5][:], F[5][:], cst["lng"][:], ALU.mult), reads=[FB[5]], writes=[FB[5]])
                yield
                S_.op("pool", TT(F[5][:], F[5][:], cst["lnb"][:], ALU.add), reads=[FB[5]], writes=[FB[5]])
                yield
                S_.op("dve", TT(h3(F[6][:]), h3(cx.v_), bc_last(bon[:], 64), ALU.mult), reads=[cx.curB, bonB, varB], writes=[FB[6]])
                yield
                S_.op("dve", TT(F[5][:], F[5][:], F[6][:], ALU.add), reads=[FB[5], FB[6]], writes=[FB[5]])
                yield
                S_.op("dve", TT(obf[:], F[5][:], cx.gt[:], ALU.mult), reads=[FB[5], cx.gtB], writes=[obfB])
                yield
                pb, pbB = pbs.next()
                for k in range(8):
                    S_.op("pe", TR(pb[:, k * 128:(k + 1) * 128], obf[:, k * 128:(k + 1) * 128], identb[:]), reads=[obfB], writes=[pbB])
                BTs, BTsB = BTr.next()
                S_.op("act", CPY(BTs[:], pb[:].rearrange("p (k t) -> p k t", t=128)), reads=[pbB], writes=[BTsB])
                yield
                S_.dma("sp", DMA(g.BT[:, cx.c * 128:(cx.c + 1) * 128].rearrange("(k p) t -> p k t", p=128), BTs[:]), reads=[BTsB], writes=[Buf()])
                yield

            yield

        def run_all(gen):
            for _ in gen:
                pass

        def interleave(g1, g2):
            done1 = done2 = False
            while not (done1 and done2):
                if not done1:
                    try:
                        next(g1)
                    except StopIteration:
                        done1 = True
                if not done2:
                    try:
                        next(g2)
                    except StopIteration:
                        done2 = True
        cxs = [Cx() for _ in range(NCHL)]
        for c_, cx in enumerate(cxs):
            cx.c = c_
        cxs[0].cur, cxs[0].curB = load_cur(0)
        load_prv(0)
        run_all(st12(cxs[0]))
        for c in range(NCHL):
            cx = cxs[c]
            cur, curB = cx.cur, cx.curB
            r_, k_, v_ = cx.r_, cx.k_, cx.v_
            if c + 1 < NCHL:
                cxs[c + 1].cur, cxs[c + 1].curB = load_cur(c + 1)
            S_.op("pool", TT(F[2][:], k_, cst["kk"][:], ALU.mult), reads=[curB], writes=[FB[2]])
            S_.op("dve", TT(F[5][:], F[2][:], F[2][:], ALU.mult), reads=[FB[2]], writes=[FB[5]])
            S_.op("dve", RED(ssk[:], h3(F[5][:])), reads=[FB[5]], writes=[sskB])
            S_.op("dve", TSC(ssk[:], ssk[:], 1e-24, None, ALU.max), reads=[sskB], writes=[sskB])
            S_.op("act", ACTF(ssk[:], ssk[:], AF.Sqrt), reads=[sskB], writes=[sskB])
            S_.op("dve", RCP(ssk[:], ssk[:]), reads=[sskB], writes=[sskB])
            S_.op("dve", TT(h3(F[2][:]), h3(F[2][:]), bc_last(ssk[:], 64), ALU.mult), reads=[FB[2], sskB], writes=[FB[2]])
            S_.op("pool", TT(F[3][:], F[2][:], F[1][:], ALU.mult), reads=[FB[2], FB[1]], writes=[FB[3]])
            S_.op("dve", STT(F[4][:], F[1][:], -1.0, cst["ka"][:], ALU.add, ALU.mult), reads=[FB[1]], writes=[FB[4]])
            S_.op("dve", STT(F[4][:], F[4][:], 1.0, k_, ALU.add, ALU.mult), reads=[FB[4], curB], writes=[FB[4]])
            S_.op("pool", TT(F[5][:], r_, F[4][:], ALU.mult), reads=[curB, FB[4], sskB], writes=[FB[5]])
            S_.op("dve", TT(F[5][:], F[5][:], cst["rk"][:], ALU.mult), reads=[FB[5]], writes=[FB[5]])
            S_.op("dve", RED(bon[:], h3(F[5][:])), reads=[FB[5]], writes=[bonB])
            S_.op("act", CPY(tk["vb"][:], v_), reads=[curB], writes=[tkB["vb"]])
            S_.op("act", CPY(sgh[:], F[0][:]), reads=[FB[0]], writes=[sghB])
            S_.op("dve", TT(F[6][:], F[0][:], sgh[:], ALU.subtract), reads=[FB[0], sghB, FB[6]], writes=[FB[6]])
            S_.op("act", CPY(sgm[:], F[6][:]), reads=[FB[6]], writes=[sgmB])

            def cums(which, scale, dstF):
                for n in range(2):
                    cs_ = slice(n * 512, (n + 1) * 512)
                    pf, pfB = pfs.next()
                    S_.op("pe", MM(pf[:, :], trib[:, which * 128:(which + 1) * 128], sgh[:, cs_], True, False), reads=[sghB], writes=[pfB])
                    S_.op("pe", MM(pf[:, :], trib[:, which * 128:(which + 1) * 128], sgm[:, cs_], False, True), reads=[sgmB], writes=[pfB])
                    S_.op("act", ACTF(F[dstF][:, cs_], pf[:, :], AF.Exp, scale=scale * CDEC), reads=[pfB, FB[dstF]], writes=[FB[dstF]])
            cums(0, 1.0, 5)
            S_.op("dve", TT(tk["rt"][:], r_, F[5][:], ALU.mult), reads=[curB, FB[5]], writes=[tkB["rt"]])
            cums(0, -1.0, 6)
            S_.op("dve", TT(tk["bt"][:], F[3][:], F[6][:], ALU.mult), reads=[FB[3], FB[6]], writes=[tkB["bt"]])
            S_.op("pool", TT(tk["kt"][:], F[4][:], F[6][:], ALU.mult), reads=[FB[4], FB[6]], writes=[tkB["kt"]])
            cums(1, 1.0, 5)
            S_.op("dve", STT(tk["at"][:], F[2][:], -1.0, F[5][:], ALU.mult, ALU.mult), reads=[FB[2], FB[5]], writes=[tkB["at"]])
            cums(2, 1.0, 6)
            S_.op("dve", TT(tk["bh"][:], F[3][:], F[6][:], ALU.mult), reads=[FB[3], FB[6]], writes=[tkB["bh"]])
            S_.op("pool", TT(tk["kh"][:], F[4][:], F[6][:], ALU.mult), reads=[FB[4], FB[6]], writes=[tkB["kh"]])
            pf, pfB = pfs.next()
            for hp in range(8):
                S_.op("pe", MM(pf[:, hp * 16:(hp + 1) * 16], sgh[:, hp * 128:(hp + 1) * 128], onesb[:], True, False), reads=[sghB], writes=[pfB])
                S_.op("pe", MM(pf[:, hp * 16:(hp + 1) * 16], sgm[:, hp * 128:(hp + 1) * 128], onesb[:], False, True), reads=[sgmB], writes=[pfB])
            S_.op("act", ACTF(PC[:], pf[:, 0:128].rearrange("p (h c) -> p h c", c=16)[:, :, 0], AF.Exp, scale=CDEC), reads=[pfB], writes=[PCB])
            for nm in ("at", "rt", "bt", "kt"):
                pb, pbB = pbs.next()
                for hp in range(8):
                    S_.op("pe", TR(pb[:, hp * 128:(hp + 1) * 128], tk[nm][:, hp * 128:(hp + 1) * 128], identb[:]), reads=[tkB[nm]], writes=[pbB])
                pb3 = pb[:].rearrange("p (k t) -> p k t", t=128)
                if nm in ("at", "rt"):
                    a_ = 0 if nm == "at" else 1
                    S_.op("act", CPY(AR[0:64, :, 0, a_, :], pb3[0:64, :, :]), reads=[pbB], writes=[ARB])
                    S_.op("dve", CPY(AR[64:128, :, 1, a_, :], pb3[64:128, :, :]), reads=[pbB], writes=[ARB])
                elif nm == "bt":
                    S_.op("act", CPY(BTf[:], pb3), reads=[pbB], writes=[BTfB])
                else:
                    S_.op("dve", CPY(KTf[:], pb3), reads=[pbB], writes=[KTfB])
            for hb_ in range(2):
                for pl in range(4):
                    hp = hb_ * 4 + pl
                    po, poB = pfs.next()
                    pk, pkB = pfs.next()
                    pn, pnB = pfs.next()
                    for half in range(2):
                        ar2 = AR[:, hp, half, :, :].rearrange("p a t -> p (a t)")
                        S_.op("pe", MM(po[:, half * 256:(half + 1) * 256], BTf[:, hp, :], ar2), reads=[BTfB, ARB], writes=[poB])
                        S_.op("pe", MM(pk[:, half * 256:(half + 1) * 256], KTf[:, hp, :], ar2), reads=[KTfB, ARB], writes=[pkB])
                        S_.op("pe", MM(pn[:, half * 128:(half + 1) * 128], AR[:, hp, half, 0, :], BTf[:, hp, :]), reads=[BTfB, ARB], writes=[pnB])
                    po3 = po[:, :].rearrange("p (h c) -> p h c", c=256)
                    pk3 = pk[:, :].rearrange("p (h c) -> p h c", c=256)
                    pn3 = pn[:, 0:256].rearrange("p (h c) -> p h c", c=128)
                    ms, mi = bc_mid(mar[:, 0:128], 2), bc_mid(mar[:, 128:256], 2)
                    S_.op("dve", TT(Nm[:, 2 * pl:2 * pl + 2, :], po3[:, :, 0:128], ms, ALU.mult), reads=[poB], writes=[NmB[pl]])
                    S_.op("dve", TT(MrbT[:, 2 * hp:2 * hp + 2, :], po3[:, :, 128:256], mi, ALU.mult), reads=[poB], writes=[MrbB])
                    S_.op("dve", TT(MakT[:, 2 * hp:2 * hp + 2, :], pk3[:, :, 0:128], ms, ALU.mult), reads=[pkB], writes=[MakB])
                    S_.op("dve", TT(MrkT[:, 2 * hp:2 * hp + 2, :], pk3[:, :, 128:256], mi, ALU.mult), reads=[pkB], writes=[MrkB])
                    S_.op("dve", TT(NT[:, 2 * pl:2 * pl + 2, :], pn3, bc_mid(mlo[:], 2), ALU.mult), reads=[pnB], writes=[NTB[pl]])
                for p2 in range(4):
                    S_.op("pool", TT(GQ[0][:, 2 * p2:2 * p2 + 2, 0, :], Nm[:, 2 * p2:2 * p2 + 2, :], bc_mid(identb[:], 2), ALU.add),
                          reads=[NmB[p2]], writes=[GQB[0][p2]])
                for q4 in range(2):
                    pq, pqB = pfs.next()
                    pt_, ptB = pfs.next()
                    for hl in range(4 * q4, 4 * q4 + 4):
                        o_ = slice((hl % 4) * 128, (hl % 4 + 1) * 128)
                        S_.op("pe", MM(pq[:, o_], NT[:, hl, :], Nm[:, hl, :]), reads=[NTB[hl // 2], NmB[hl // 2]], writes=[pqB])
                        S_.op("pe", MM(pt_[:, o_], Nm[:, hl, :], NT[:, hl, :]), reads=[NTB[hl // 2], NmB[hl // 2]], writes=[ptB])
                    S_.op("act", CPY(GQ[0][:, 4 * q4:4 * q4 + 4, 1, :], pq[:, :].rearrange("p (h c) -> p h c", c=128)), reads=[pqB], writes=[GQB[0][2 * q4], GQB[0][2 * q4 + 1]])
                    S_.op("act", CPY(QTt[0][:, 4 * q4:4 * q4 + 4, :], pt_[:, :].rearrange("p (h c) -> p h c", c=128)), reads=[ptB], writes=[QTB[0][q4]])
                for i in range(1, 7):
                    ci, ni = (i - 1) % 2, i % 2
                    last = (i == 6)
                    if not last:
                        for p2 in range(4):
                            pa, paB = pfs.next()
                            for hl in (2 * p2, 2 * p2 + 1):
                                o0 = (hl % 2) * 256
                                rd = [QTB[ci][hl // 4], GQB[ci][hl // 2]]
                                S_.op("pe", MM(pa[:, o0 + 128:o0 + 256], QTt[ci][:, hl, :], GQ[ci][:, hl, 1, :], True, True), reads=rd, writes=[paB])
                                S_.op("pe", MM(pa[:, o0:o0 + 128], QTt[ci][:, hl, :], GQ[ci][:, hl, 0, :], True, False), reads=rd, writes=[paB])
                                S_.op("pe", MM(pa[:, o0:o0 + 128], identb[:], GQ[ci][:, hl, 0, :], False, True), reads=rd, writes=[paB])
                            S_.op("dve" if p2 % 2 == 0 else "act",
                                  CPY(GQ[ni][:, 2 * p2:2 * p2 + 2, :, :].rearrange("p h a t -> p h (a t)"), pa[:, :].rearrange("p (h c) -> p h c", c=256)),
                                  reads=[paB, GQB[ci][p2]], writes=[GQB[ni][p2]])
                        for q4 in range(2):
                            pt_, ptB = pfs.next()
                            for hl in range(4 * q4, 4 * q4 + 4):
                                S_.op("pe", MM(pt_[:, (hl % 4) * 128:(hl % 4 + 1) * 128], GQ[ci][:, hl, 1, :], QTt[ci][:, hl, :]),
                                      reads=[QTB[ci][hl // 4], GQB[ci][hl // 2]], writes=[ptB])
                            S_.op("act", CPY(QTt[ni][:, 4 * q4:4 * q4 + 4, :], pt_[:, :].rearrange("p (h c) -> p h c", c=128)), reads=[ptB], writes=[QTB[ni][q4]])
                    else:
                        for q4 in range(2):
                            pa, paB = pfs.next()
                            for hl in range(4 * q4, 4 * q4 + 4):
                                S_.op("pe", MM(pa[:, (hl % 4) * 128:(hl % 4 + 1) * 128], QTt[ci][:, hl, :], GQ[ci][:, hl, 0, :]),
                                      reads=[QTB[ci][hl // 4], GQB[ci][hl // 2]], writes=[paB])
                            S_.op("dve", TT(G7[:, hb_ * 8 + 4 * q4:hb_ * 8 + 4 * q4 + 4, :], pa[:, :].rearrange("p (h c) -> p h c", c=128),
                                            GQ[ci][:, 4 * q4:4 * q4 + 4, 0, :], ALU.add), reads=[paB, GQB[ci][2 * q4], GQB[ci][2 * q4 + 1]], writes=[G7B])

            if c + 1 < NCHL:
                interleave(st78(cx), st12(cxs[c + 1]))
            else:
                run_all(st78(cx))
        S_.barrier()
        S_.emit()
```

```python
import numpy as np
import concourse.bass as bass
import concourse.mybir as mybir

F32 = mybir.dt.float32
BF16 = mybir.dt.bfloat16
I32 = mybir.dt.int32
AF = mybir.ActivationFunctionType
ALU = mybir.AluOpType
AX = mybir.AxisListType

COMPUTE = ("pe", "act", "dve", "pool")
NDMA_SEM = 12


class Buf:
    __slots__ = ("name", "last_w", "readers")

    def __init__(self, name=""):
        self.name = name
        self.last_w = None
        self.readers = []


class Op:
    __slots__ = ("eng", "fn", "idx", "deps", "signal", "is_dma", "dma_k", "cnt")

    def __init__(self, eng, fn, idx, is_dma):
        self.eng = eng
        self.fn = fn
        self.idx = idx
        self.deps = []
        self.signal = False
        self.is_dma = is_dma
        self.dma_k = -1
        self.cnt = 0


class Sched:
    def __init__(self, nc):
        self.nc = nc
        self.engs = ("sp", "act", "dve", "pe", "pool")
        self.ops = {e: [] for e in self.engs}
        self.nops = {e: 0 for e in self.engs}
        self.sigcnt = {e: 0 for e in self.engs}
        self.dmacnt = {e: 0 for e in self.engs}
        self.seen = {e: {o: -1 for o in self.engs} for e in self.engs}
        self.seen_dma = {e: set() for e in self.engs}
        self.csem = {}
        self.dsem = {}
        self.pending_dma = []
        self.all_dma = {e: [] for e in self.engs}

    def alloc_sems(self, stack):
        for e in COMPUTE:
            self.csem[e] = stack.enter_context(self.nc.semaphore("c_" + e))
        for q in ("sp", "pool", "act"):
            self.dsem[q] = [stack.enter_context(self.nc.semaphore("d_%s%d" % (q, i))) for i in range(NDMA_SEM)]

    def _add(self, eng, fn, reads, writes, is_dma):
        o = Op(eng, fn, self.nops[eng], is_dma)
        self.nops[eng] += 1
        deps = {}
        for b in reads:
            if b.last_w is not None:
                deps[id(b.last_w)] = b.last_w
        for b in writes:
            if b.last_w is not None:
                deps[id(b.last_w)] = b.last_w
            for r in b.readers:
                deps[id(r)] = r
        best = {}
        for d in deps.values():
            if d is o:
                continue
            if d.is_dma:
                if id(d) in self.seen_dma[eng]:
                    continue
                self.seen_dma[eng].add(id(d))
                o.deps.append(d)
            else:
                if d.eng == eng and eng == "pe":
                    continue
                if d.idx <= self.seen[eng][d.eng]:
                    continue
                if d.eng not in best or best[d.eng].idx < d.idx:
                    best[d.eng] = d
        for d in best.values():
            self.seen[eng][d.eng] = d.idx
            d.signal = True
            o.deps.append(d)
        for b in reads:
            b.readers.append(o)
        for b in writes:
            b.last_w = o
            b.readers = []
        if is_dma:
            o.dma_k = self.dmacnt[eng]
            self.dmacnt[eng] += 1
            self.all_dma[eng].append(o)
        self.ops[eng].append(o)
        return o

    def op(self, eng, fn, reads=(), writes=()):
        return self._add(eng, fn, reads, writes, False)

    def dma(self, q, fn, reads=(), writes=()):
        return self._add(q, fn, reads, writes, True)

    def barrier(self):
        lasts = {}
        for e in COMPUTE:
            for o in reversed(self.ops[e]):
                if (not o.is_dma) and o.fn is not None:
                    lasts[e] = o
                    break
        dmas = []
        for q in self.engs:
            n = len(self.all_dma[q])
            dmas.extend(self.all_dma[q][max(0, n - NDMA_SEM):])
        for e in self.engs:
            o = Op(e, None, self.nops[e], False)
            self.nops[e] += 1
            for e2, d in lasts.items():
                if e2 == e:
                    continue
                if d.idx > self.seen[e][e2]:
                    self.seen[e][e2] = d.idx
                    d.signal = True
                    o.deps.append(d)
            for d in dmas:
                if id(d) not in self.seen_dma[e]:
                    self.seen_dma[e].add(id(d))
                    o.deps.append(d)
            self.ops[e].append(o)

    def _emit_engine(self, ename, e):
        for o in self.ops[ename]:
            for d in o.deps:
                if d.is_dma:
                    sem = self.dsem[d.eng][d.dma_k % NDMA_SEM]
                    e.wait_ge(sem, 16 * (d.dma_k // NDMA_SEM + 1))
                else:
                    e.wait_ge(self.csem[d.eng], d.cnt)
            if o.fn is None:
                continue
            if o.is_dma:
                k = o.dma_k
                sem = self.dsem[ename][k % NDMA_SEM]
                if k >= NDMA_SEM:
                    e.wait_ge(sem, 16 * (k // NDMA_SEM))
                ins = o.fn(e)
                ins.then_inc(sem, 16)
            else:
                ins = o.fn(e)
                if o.signal:
                    ins.then_inc(self.csem[ename], 1)

    def emit(self, name=None):
        for e in COMPUTE:
            c = self.sigcnt[e]
            for o in self.ops[e]:
                if (not o.is_dma) and o.signal:
                    c += 1
                    o.cnt = c
            self.sigcnt[e] = c
        with self.nc.Block() as block:
            @block.sync
            def _(e):
                self._emit_engine("sp", e)

            @block.scalar
            def _(e):
                self._emit_engine("act", e)

            @block.vector
            def _(e):
                self._emit_engine("dve", e)

            @block.tensor
            def _(e):
                self._emit_engine("pe", e)

            @block.gpsimd
            def _(e):
                self._emit_engine("pool", e)
        self.ops = {e: [] for e in self.engs}


import contextlib
import math
import ml_dtypes
from concourse.bass_utils import run_bass_kernel_spmd

D = 2048
NH = 8
QR, KVR, ROPE, NOPE, DV = 768, 512, 64, 128, 128
QK = NOPE + ROPE
RW = 1024
RH, RN = 16, 64
RIN = 3360
INW = 4704
FF = 5632
EPS = 1e-6
GN_EPS = 64e-5
TS = 512
CDEC = -math.exp(-0.5)


def own_tiles(S, c):
    nt = S // TS
    out = []
    for j in range(nt // 2):
        first = (j % 2 == 0)
        if c == 1:
            first = not first
        out.append(2 * j if first else 2 * j + 1)
    return out


class Ctx:
    pass


def build_nc(S, debug=False):
    nc = bass.Bass("TRN2", target_bir_lowering=False)
    SO = S // 2
    NT = S // TS
    NSL = NT // 2
    NCH = S // 128
    kind_dbg = "ExternalOutput" if debug else "Internal"

    def din(name, shape, dt=F32):
        return nc.dram_tensor(name, list(shape), dt, kind="ExternalInput").ap()

    def dscr(name, shape, dt, dbg=False):
        return nc.dram_tensor(name, list(shape), dt, kind=(kind_dbg if dbg else "Internal")).ap()

    g = Ctx()
    g.nc, g.S, g.SO, g.NT, g.NSL, g.NCH = nc, S, SO, NT, NSL, NCH
    g.x_true = din("x_true", [S, D])
    g.x_own = din("x_own", [SO, D])
    g.cs_true = din("cs_true", [S, 64])
    g.cs_own = din("cs_own", [SO, 64])
    g.sel = din("sel", [1, 2 * NSL])
    g.attn_norm_g = din("attn_norm_g", [1, D])
    g.w_in = din("w_in", [D, INW])
    g.q_lat_norm = din("q_lat_norm", [1, QR])
    g.w_uq = din("w_uq", [QR, NH * QK])
    g.kv_lat_norm = din("kv_lat_norm", [1, KVR])
    g.w_ukv = din("w_ukv", [KVR, NH * 256])
    g.q_head_norm = din("q_head_norm", [1, QK])
    g.k_nope_norm = din("k_nope_norm", [1, NOPE])
    g.k_rope_norm = din("k_rope_norm", [1, ROPE])
    g.shift_mix = din("rwkv_shift_mix", [1, RIN])
    g.w0 = din("rwkv_w0", [1, RW])
    g.w2 = din("rwkv_w2", [64, RW])
    g.a0 = din("rwkv_a0", [1, RW])
    g.a2 = din("rwkv_a2", [64, RW])
    g.g2 = din("rwkv_g2", [160, RW])
    g.k_k = din("rwkv_k_k", [1, RW])
    g.k_a = din("rwkv_k_a", [1, RW])
    g.r_k = din("rwkv_r_k", [1, RW])
    g.ln_g = din("rwkv_ln_g", [1, RW])
    g.ln_b = din("rwkv_ln_b", [1, RW])
    g.w_out = din("w_out", [D, D])
    g.ffn_norm_g = din("ffn_norm_g", [1, D])
    g.w_gate = din("w_gate", [D, FF])
    g.w_up = din("w_up", [D, FF])
    g.w_down = din("w_down", [FF, D])
    g.ident = din("ident", [128, 128])
    g.diagmask = din("diagmask", [128, 4 * 512])
    g.rmask_ar = din("rmask_ar", [128, 256])
    g.rmask_lo = din("rmask_lo", [128, 128])
    g.tri = din("tri", [128, 3 * 128 + 1])
    g.out = nc.dram_tensor("out", [SO, D], F32, kind="ExternalOutput").ap()
    g.wbA = dscr("wbA", [9, 128, 16 * 512], BF16)
    g.ukvb = dscr("ukvb", [128, 4 * 2048], BF16)
    g.wqb = dscr("wqb", [128, 16 * 768], BF16)
    g.uqb = dscr("uqb", [128, 6 * 1536], BF16)
    g.w2b = dscr("w2b", [64, 1024], BF16)
    g.a2b = dscr("a2b", [64, 1024], BF16)
    g.g2b = dscr("g2b", [160, 1024], BF16)
    g.wbO = dscr("wbO", [8, 128, 16 * 256], BF16)
    g.wbGU = dscr("wbGU", [44, 128, 2 * 16 * 128], BF16)
    g.wbD = dscr("wbD", [16, 128, 11 * 512], BF16)
    g.KT = dscr("KT", [NH, 128, S], BF16, True)
    g.KPT = dscr("KPT", [64, S], BF16, True)
    g.V = dscr("V", [S, NH * DV], BF16, True)
    g.QT = dscr("QT", [NH, 128, SO], BF16, True)
    g.QPT = dscr("QPT", [NH, 64, SO], BF16, True)
    g.PR = dscr("PR", [S, RIN], F32, True)
    g.AT = dscr("AT", [1024, SO], BF16, True)
    g.BT = dscr("BT", [1024, S], BF16, True)

    g.ablocks = [(768, 512), (1280, 64)] + [(1344 + 512 * i, min(512, RIN - 512 * i)) for i in range(7)]
    S_ = Sched(nc)
    g.sch = S_
    g.wb = {k: Buf(k) for k in ("wbA", "wbO", "wbGU", "wbD", "KT", "KPT", "V", "QT", "QPT", "PR", "AT", "BT")}
    g.wbA_b = [Buf("wbA%d" % i) for i in range(9)]
    with contextlib.ExitStack() as st:
        S_.alloc_sems(st)
        import os
        ph = os.environ.get("KPH", "cast,a1,a2,c,d,e").split(",")
        if "cast" in ph:
            phase_cast(g)
        if "a1" in ph:
            phase_a1(g)
        if "a2" in ph:
            phase_a2(g)
        if "c" in ph:
            phase_c(g)
        if "d" in ph:
            phase_d(g)
        if "e" in ph:
            phase_e(g)
        S_.barrier()
        S_.emit()
    return nc


def MM(out, l, r, st=True, sp=True):
    return lambda e: e.matmul(out, l, r, start=st, stop=sp)


def TR(out, in_, ident):
    return lambda e: e.transpose(out, in_, ident)


def ACTF(out, in_, func, **kw):
    return lambda e: e.activation(out=out, in_=in_, func=func, **kw)


def CPY(out, in_):
    def f(e):
        if hasattr(e, "tensor_copy"):
            return e.tensor_copy(out, in_)
        return e.activation(out=out, in_=in_, func=AF.Copy)
    return f


def TT(out, a, b, op):
    return lambda e: e.tensor_tensor(out, a, b, op)


def TSC(out, a, s1, s2, op0, op1=None):
    if op1 is None:
        return lambda e: e.tensor_scalar(out, a, s1, None, op0)
    return lambda e: e.tensor_scalar(out, a, s1, s2, op0, op1)


def STT(out, a, s, b, op0, op1):
    return lambda e: e.scalar_tensor_tensor(out, a, s, b, op0, op1)


def RED(out, in_, op=None):
    return lambda e: e.tensor_reduce(out, in_, AX.X, ALU.add if op is None else op)


def RCP(out, in_):
    return lambda e: e.reciprocal(out, in_)


def DMA(out, in_):
    return lambda e: e.dma_start(out=out, in_=in_)


def MEMSET(ap, v):
    return lambda e: e.memset(ap, v)


def rstd_ops(S_, ss, n, eps, B):
    S_.op("dve", TSC(ss, ss, 1.0 / n, eps, ALU.mult, ALU.add), reads=[B], writes=[B])
    S_.op("act", ACTF(ss, ss, AF.Sqrt), reads=[B], writes=[B])
    S_.op("dve", RCP(ss, ss), reads=[B], writes=[B])


def bc_last(ap, n):
    sh = list(ap.shape)
    return ap.unsqueeze(len(sh)).broadcast_to(sh + [n])


def bc_mid(ap, n):
    sh = list(ap.shape)
    return ap.unsqueeze(1).broadcast_to([sh[0], n] + sh[1:])


class Ring:
    def __init__(self, items):
        self.items = items
        self.i = 0

    def next(self):
        it = self.items[self.i % len(self.items)]
        self.i += 1
        return it


def phase_cast(g):
    nc, S_ = g.nc, g.sch
    jobs = []

    def blk(w, r0, nk, c0, cb):
        return w[r0:r0 + nk * 128, c0:c0 + cb].rearrange("(k p) c -> p k c", p=128)
    for b, (c0, cb) in enumerate(g.ablocks):
        jobs.append(([(blk(g.w_in, 0, 16, c0, cb), 0, 16, cb)], g.wbA[b, :, 0:16 * cb], 128, 16 * cb))
    jobs.append(([(blk(g.w_ukv, 0, 4, 0, 2048), 0, 4, 2048)], g.ukvb[:, :], 128, 8192))
    for hf in range(2):
        jobs.append(([(blk(g.w_in, 0, 16, hf * 384, 384), 0, 16, 384)],
                     g.wqb.rearrange("p (k c) -> p k c", c=768)[:, :, hf * 384:(hf + 1) * 384], 128, 16 * 384))
    for hf in range(2):
        jobs.append(([(blk(g.w_uq, 0, 6, hf * 768, 768), 0, 6, 768)],
                     g.uqb.rearrange("p (k c) -> p k c", c=1536)[:, :, hf * 768:(hf + 1) * 768], 128, 6 * 768))
    jobs.append(([(g.w2[:, :], 0, 1, 1024)], g.w2b[:, :], 64, 1024))
    jobs.append(([(g.a2[:, :], 0, 1, 1024)], g.a2b[:, :], 64, 1024))
    jobs.append(([(g.g2[0:128, :], 0, 1, 1024)], g.g2b[0:128, :], 128, 1024))
    jobs.append(([(g.g2[128:160, :], 0, 1, 1024)], g.g2b[128:160, :], 32, 1024))
    for b in range(8):
        jobs.append(([(blk(g.w_out, 0, 16, b * 256, 256), 0, 16, 256)], g.wbO[b], 128, 4096))
    for hc in range(44):
        jobs.append(([(blk(g.w_gate, 0, 16, hc * 128, 128), 0, 16, 128), (blk(g.w_up, 0, 16, hc * 128, 128), 2048, 16, 128)],
                     g.wbGU[hc], 128, 4096))
    for fb in range(4):
        for hg in range(4):
            jobs.append(([(blk(g.w_down, hg * 11 * 128, 11, fb * 512, 512), 0, 11, 512)], g.wbD[fb * 4 + hg], 128, 5632))
    with contextlib.ExitStack() as s1:
        fr = Ring([(s1.enter_context(nc.sbuf_tensor("cf%d" % i, [128, 8192], F32)), Buf()) for i in range(3)])
        br = Ring([(s1.enter_context(nc.sbuf_tensor("cb%d" % i, [128, 8192], BF16)), Buf()) for i in range(3)])
        engs = ["act", "dve", "pool"]

        def issue_load(j):
            srcs, dst, P, n = jobs[j]
            f, fB = fr.next()
            for (src, off, a, b_) in srcs:
                if a == 1:
                    S_.dma("sp", DMA(f[0:P, off:off + b_], src), writes=[fB])
                else:
                    S_.dma("sp", DMA(f[0:P, off:off + a * b_].rearrange("p (k c) -> p k c", c=b_), src), writes=[fB])
            return f, fB
        pend = [issue_load(0), issue_load(1)]
        for j in range(len(jobs)):
            srcs, dst, P, n = jobs[j]
            f, fB = pend.pop(0)
            if j + 2 < len(jobs):
                pend.append(issue_load(j + 2))
            bt, bB = br.next()
            S_.op(engs[j % 3], CPY(bt[0:P, 0:n], f[0:P, 0:n]), reads=[fB], writes=[bB])
            if len(dst.shape) == 3:
                S_.dma("sp", DMA(dst, bt[0:P, 0:n].rearrange("p (k c) -> p k c", c=dst.shape[2])), reads=[bB], writes=[Buf()])
            else:
                S_.dma("sp", DMA(dst, bt[0:P, 0:n]), reads=[bB], writes=[Buf()])
        S_.barrier()
        S_.emit()


def x_to_hT(g, S_, xs, xsB, gbc, hb, hbB, junk, junkB, ss, ssB, identb, pbs, hT, hTB, col0, n_rms=D):
    S_.op("act", ACTF(junk, xs, AF.Square, accum_out=ss), reads=[xsB], writes=[junkB, ssB])
    rstd_ops(S_, ss, n_rms, EPS, ssB)
    S_.op("dve", STT(hb, xs, ss, gbc, ALU.mult, ALU.mult), reads=[xsB, ssB], writes=[hbB])
    for half in range(2):
        pb, pbB = pbs.next()
        for k in range(8):
            kk = half * 8 + k
            S_.op("pe", TR(pb[:, k * 128:(k + 1) * 128], hb[:, kk * 128:(kk + 1) * 128], identb), reads=[hbB], writes=[pbB])
        S_.op("act" if half == 0 else "dve",
              CPY(hT[:, half * 8:(half + 1) * 8, col0:col0 + 128], pb[:].rearrange("p (k t) -> p k t", t=128)),
              reads=[pbB], writes=[hTB])


def phase_a1(g):
    nc, S_ = g.nc, g.sch
    with contextlib.ExitStack() as s1:
        def sb(n, sh, dt):
            return s1.enter_context(nc.sbuf_tensor("a1_" + n, sh, dt))

        def ps(n, sh, dt):
            return s1.enter_context(nc.psum_tensor("a1_" + n, sh, dt))
        xr = Ring([(sb("xs%d" % i, [128, D], F32), Buf()) for i in range(3)])
        gbc = sb("gbc", [128, D], F32)
        hbr = Ring([(sb("hb%d" % i, [128, D], BF16), Buf()) for i in range(2)])
        junk = sb("junk", [128, D], BF16); junkB = Buf()
        hTr = Ring([(sb("hT%d" % i, [128, 16, TS], BF16), Buf()) for i in range(2)])
        wr = Ring([(sb("wblk%d" % i, [128, 16 * 512], BF16), Buf()) for i in range(2)])
        lat = sb("lat", [128, 4, 576], F32); latB = [Buf() for _ in range(4)]
        prr = Ring([(sb("prst%d" % i, [128, 512], F32), Buf()) for i in range(3)])
        ukv = sb("ukv", [128, 4, 2048], BF16)
        kvn = sb("kvn", [128, 512], BF16); kvnB = Buf()
        kvnT = sb("kvnT", [128, 4, 128], BF16); kvnTB = Buf()
        kvs = sb("kvs", [128, 2048], F32); kvsB = Buf()
        sq = sb("sq", [128, 1024], F32); sqB = Buf()
        kn = sb("kn", [128, 8, 128], BF16); knB = Buf()
        KTr = Ring([(sb("KTst%d" % i, [128, 8, TS], BF16), Buf()) for i in range(2)])
        Vr = Ring([(sb("Vst%d" % i, [128, 4, 1024], BF16), Buf()) for i in range(2)])
        KPr = Ring([(sb("KPst%d" % i, [64, TS], BF16), Buf()) for i in range(2)])
        csr = Ring([(sb("cs%d" % i, [128, 4, 64], F32), Buf()) for i in range(2)])
        gkv = sb("gkv", [128, 512], F32)
        gkn = sb("gkn", [128, 128], F32)
        gkr = sb("gkr", [128, 64], F32)
        identf = sb("identf", [128, 128], F32)
        identb = sb("identb", [128, 128], BF16)
        ssr = Ring([(sb("ss%d" % i, [128, 1], F32), Buf()) for i in range(4)])
        ssk = sb("ssk", [128, 8], F32); sskB = Buf()
        krn = sb("krn", [128, 64], F32); krnB = Buf()
        kr2 = sb("kr2", [128, 64], F32); kr2B = Buf()
        kpe = sb("kpe", [128, 64], BF16); kpeB = Buf()
        pfs = Ring([(ps("pf%d" % i, [128, 512], F32), Buf()) for i in range(6)])
        pbs = Ring([(ps("pb%d" % i, [128, 1024], BF16), Buf()) for i in range(2)])
        cB = Buf()
        S_.dma("sp", DMA(gbc[:], g.attn_norm_g[0:1, :].partition_broadcast(128)), writes=[cB])
        S_.dma("sp", DMA(gkv[:], g.kv_lat_norm[0:1, :].partition_broadcast(128)), writes=[cB])
        S_.dma("sp", DMA(gkn[:], g.k_nope_norm[0:1, :].partition_broadcast(128)), writes=[cB])
        S_.dma("sp", DMA(gkr[:], g.k_rope_norm[0:1, :].partition_broadcast(128)), writes=[cB])
        S_.dma("sp", DMA(identf[:], g.ident[:, :]), writes=[cB])
        S_.op("dve", CPY(identb[:], identf[:]), reads=[cB], writes=[cB])
        S_.dma("sp", DMA(ukv[:].rearrange("p k c -> p (k c)"), g.ukvb[:, :]), writes=[cB])
        S_.barrier()

        def load_x(T, s):
            xs, xsB = xr.next()
            t0 = T * TS + s * 128
            S_.dma("sp", DMA(xs[:], g.x_true[t0:t0 + 128, :]), writes=[xsB])
            return xs, xsB

        def load_w(b):
            w, wB = wr.next()
            cb_ = g.ablocks[b][1]
            S_.dma("sp", DMA(w[:, 0:16 * cb_], g.wbA[b, :, 0:16 * cb_]), writes=[wB])
            return w, wB

        pend = [load_x(0, 0), load_x(0, 1)]
        wpend = [load_w(0)]
        for T in range(g.NT):
            hT, hTB = hTr.next()
            cs, csB = csr.next()
            S_.dma("sp", DMA(cs[:], g.cs_true[T * TS:(T + 1) * TS, :].rearrange("(s p) c -> p s c", p=128)), writes=[csB])
            for s in range(4):
                xs, xsB = pend.pop(0)
                nxt = T * 4 + s + 2
                if nxt < g.NT * 4:
                    pend.append(load_x(nxt // 4, nxt % 4))
                hb, hbB = hbr.next()
                ss, ssB = ssr.next()
                x_to_hT(g, S_, xs[:], xsB, gbc[:], hb[:], hbB, junk[:], junkB, ss[:], ssB, identb[:], pbs, hT, hTB, s * 128)
            KTst, KTB = KTr.next()
            Vst, VB = Vr.next()
            KPst, KPB = KPr.next()
            import os
            lvl = int(os.environ.get("A1STOP", "9"))
            if lvl < 2:
                continue
            for b, (c0, cb) in enumerate(g.ablocks):
                w, wB = wpend.pop(0)
                nb_ = (b + 1) % len(g.ablocks)
                if not (T == g.NT - 1 and b == len(g.ablocks) - 1):
                    wpend.append(load_w(nb_))
                wv = w[:, 0:16 * cb].rearrange("p (k c) -> p k c", c=cb)
                for s in range(4):
                    pf, pfB = pfs.next()
                    for k in range(16):
                        S_.op("pe", MM(pf[:, 0:cb], hT[:, k, s * 128:(s + 1) * 128], wv[:, k, :], k == 0, k == 15),
                              reads=[hTB, wB], writes=[pfB])
                    if b == 0:
                        S_.op("act", CPY(lat[:, s, 0:512], pf[:, :]), reads=[pfB], writes=[latB[s]])
                    elif b == 1:
                        S_.op("act", CPY(lat[:, s, 512:576], pf[:, 0:64]), reads=[pfB], writes=[latB[s]])
                    else:
                        pr, prB = prr.next()
                        S_.op("act" if (s % 2 == 0) else "dve", CPY(pr[:, 0:cb], pf[:, 0:cb]), reads=[pfB], writes=[prB])
                        t0 = T * TS + s * 128
                        S_.dma("sp", DMA(g.PR[t0:t0 + 128, c0 - 1344:c0 - 1344 + cb], pr[:, 0:cb]), reads=[prB], writes=[Buf()])
                if b >= 1 and lvl >= 3:
                    def seg1(s):
                        ss, ssB = ssr.next()
                        S_.op("act", ACTF(junk[:, 0:512], lat[:, s, 0:512], AF.Square, accum_out=ss[:]),
                              reads=[latB[s]], writes=[junkB, ssB])
                        rstd_ops(S_, ss[:], KVR, EPS, ssB)
                        S_.op("dve", STT(kvn[:], lat[:, s, 0:512], ss[:], gkv[:], ALU.mult, ALU.mult),
                              reads=[latB[s], ssB], writes=[kvnB])

                    def seg2(s):
                        pb, pbB = pbs.next()
                        for k in range(4):
                            S_.op("pe", TR(pb[:, k * 128:(k + 1) * 128], kvn[:, k * 128:(k + 1) * 128], identb[:]),
                                  reads=[kvnB], writes=[pbB])
                        S_.op("act", CPY(kvnT[:], pb[:, 0:512].rearrange("p (k t) -> p k t", t=128)), reads=[pbB], writes=[kvnTB])
                        for n in range(4):
                            pf, pfB = pfs.next()
                            for k in range(4):
                                S_.op("pe", MM(pf[:, :], kvnT[:, k, :], ukv[:, k, n * 512:(n + 1) * 512], k == 0, k == 3),
                                      reads=[kvnTB], writes=[pfB])
                            S_.op("act" if n % 2 == 0 else "dve", CPY(kvs[:, n * 512:(n + 1) * 512], pf[:, :]),
                                  reads=[pfB], writes=[kvsB])
                        kv3 = kvs[:].rearrange("p (h c) -> p h c", c=256)
                        S_.op("dve", CPY(Vst[:, s, :].rearrange("p (h c) -> p h c", c=128), kv3[:, :, 128:256]),
                              reads=[kvsB], writes=[VB])
                        sq3 = sq[:].rearrange("p (h c) -> p h c", c=128)
                        S_.op("dve", TT(sq3, kv3[:, :, 0:128], kv3[:, :, 0:128], ALU.mult), reads=[kvsB], writes=[sqB])
                        S_.op("dve", RED(ssk[:], sq3), reads=[sqB], writes=[sskB])
                        rstd_ops(S_, ssk[:], NOPE, EPS, sskB)
                        S_.op("dve", TT(sq3, kv3[:, :, 0:128], bc_last(ssk[:], 128), ALU.mult), reads=[kvsB, sskB], writes=[sqB])
                        S_.op("dve", TT(kn[:], sq3, bc_mid(gkn[:], 8), ALU.mult), reads=[sqB], writes=[knB])
                        ss, ssB = ssr.next()
                        S_.op("act", ACTF(junk[:, 0:64], lat[:, s, 512:576], AF.Square, accum_out=ss[:]),
                              reads=[latB[s]], writes=[junkB, ssB])
                        rstd_ops(S_, ss[:], ROPE, EPS, ssB)
                        S_.op("dve", STT(krn[:], lat[:, s, 512:576], ss[:], gkr[:], ALU.mult, ALU.mult),
                              reads=[latB[s], ssB], writes=[krnB])
                        rope_ops(S_, krn[:].rearrange("p (o c) -> p o c", o=1), krnB, kr2[:].rearrange("p (o c) -> p o c", o=1), kr2B,
                                 kpe[:].rearrange("p (o c) -> p o c", o=1), kpeB, cs[:, s, :], csB, 1)

                    def seg3(s):
                        pb, pbB = pbs.next()
                        for h in range(8):
                            S_.op("pe", TR(pb[:, h * 128:(h + 1) * 128], kn[:, h, :], identb[:]), reads=[knB], writes=[pbB])
                        S_.op("act", CPY(KTst[:, :, s * 128:(s + 1) * 128], pb[:].rearrange("p (h t) -> p h t", t=128)),
                              reads=[pbB], writes=[KTB])
                        pb, pbB = pbs.next()
                        S_.op("pe", TR(pb[0:64, 0:128], kpe[:], identb[:]), reads=[kpeB], writes=[pbB])
                        S_.op("act", CPY(KPst[:, s * 128:(s + 1) * 128], pb[0:64, 0:128]), reads=[pbB], writes=[KPB])
                    st_ = b - 1
                    if 0 <= st_ - 2 < 4:
                        seg3(st_ - 2)
                    if 0 <= st_ - 1 < 4:
                        seg2(st_ - 1)
                    if 0 <= st_ < 4:
                        seg1(st_)
                    if st_ == 5:
                        c0t = T * TS
                        S_.dma("sp", DMA(g.KT[:, :, c0t:c0t + TS].rearrange("h d t -> d h t"), KTst[:]), reads=[KTB], writes=[Buf()])
                        S_.dma("sp", DMA(g.V[c0t:c0t + TS, :].rearrange("(s p) f -> p s f", p=128), Vst[:]), reads=[VB], writes=[Buf()])
                        S_.dma("sp", DMA(g.KPT[:, c0t:c0t + TS], KPst[:]), reads=[KPB], writes=[Buf()])
        S_.barrier()
        S_.emit()


def rope_ops(S_, x, xB, tmp, tmpB, out, outB, cs, csB, nh):
    cos = bc_mid(cs[:, 0:32], nh)
    sin = bc_mid(cs[:, 32:64], nh)
    x1, x2 = x[:, :, 0:32], x[:, :, 32:64]
    t1, t2 = tmp[:, :, 0:32], tmp[:, :, 32:64]
    S_.op("dve", TT(t1, x2, sin, ALU.mult), reads=[xB, csB], writes=[tmpB])
    S_.op("dve", TT(t2, x1, sin, ALU.mult), reads=[xB, csB], writes=[tmpB])
    S_.op("dve", TT(x1, x1, cos, ALU.mult), reads=[xB, csB, tmpB], writes=[xB])
    S_.op("dve", TT(x2, x2, cos, ALU.mult), reads=[xB, csB, tmpB], writes=[xB])
    S_.op("dve", TT(out[:, :, 0:32], x1, t1, ALU.subtract), reads=[xB, tmpB], writes=[outB])
    S_.op("dve", TT(out[:, :, 32:64], x2, t2, ALU.add), reads=[xB, tmpB], writes=[outB])


def host_consts(S):
    c = {}
    c["ident"] = np.eye(128, dtype=np.float32)
    p = np.arange(128)[:, None, None]
    kb = np.arange(4)[None, :, None]
    q = np.arange(512)[None, None, :]
    c["diagmask"] = ((kb * 128 + p) <= q).astype(np.float32).reshape(128, 2048)
    s = np.arange(128)[:, None]
    t = np.arange(128)[None, :]
    c["rmask_ar"] = np.concatenate([(s < t), (s <= t)], axis=1).astype(np.float32)
    c["rmask_lo"] = (s > t).astype(np.float32)
    tri = np.concatenate([(s <= t), (s < t), (s > t), np.ones((128, 1), bool)], axis=1).astype(np.float32)
    c["tri"] = tri
    return c


def rope_table(pos):
    inv_freq = (1.0 / (10000.0 ** (np.arange(0, 64, 2, dtype=np.float32) / np.float32(64)))).astype(np.float32)
    ang = pos.astype(np.float32)[:, None] * inv_freq[None, :]
    return np.concatenate([np.cos(ang), np.sin(ang)], axis=1).astype(np.float32)


def make_in_maps(inputs, S, n_batch):
    consts = host_consts(S)
    maps = []
    names = ["attn_norm_g", "w_in", "q_lat_norm", "w_uq", "kv_lat_norm", "w_ukv", "q_head_norm", "k_nope_norm",
             "k_rope_norm", "rwkv_shift_mix", "rwkv_w0", "rwkv_w2", "rwkv_a0", "rwkv_a2", "rwkv_g2", "rwkv_k_k",
             "rwkv_k_a", "rwkv_r_k", "rwkv_ln_g", "rwkv_ln_b", "w_out", "ffn_norm_g", "w_gate", "w_up", "w_down"]
    shared = {}
    for n in names:
        a = np.asarray(inputs[n])[0]
        if a.ndim == 1:
            a = a[None, :]
        if n == "rwkv_r_k":
            a = a.reshape(1, RW)
        shared[n] = np.ascontiguousarray(a, dtype=np.float32)
    shared.update(consts)
    x = np.asarray(inputs["x"])
    NSL = S // TS // 2
    for b in range(n_batch):
        for c in range(2):
            m = dict(shared)
            own = own_tiles(S, c)
            m["x_true"] = np.ascontiguousarray(x[b])
            m["x_own"] = np.ascontiguousarray(np.concatenate([x[b, t * TS:(t + 1) * TS] for t in own], axis=0))
            pos_own = np.concatenate([np.arange(t * TS, (t + 1) * TS) for t in own])
            m["cs_true"] = rope_table(np.arange(S))
            m["cs_own"] = rope_table(pos_own)
            sfirst = np.array([1.0 if own[j] == 2 * j else 0.0 for j in range(NSL)], np.float32)
            m["sel"] = np.concatenate([sfirst, 1.0 - sfirst])[None, :].astype(np.float32)
            maps.append(m)
    return maps


_NC_CACHE = {}


def kernel(**inputs):
    x = np.asarray(inputs["x"])
    B, S, _ = x.shape
    if S not in _NC_CACHE:
        _NC_CACHE[S] = build_nc(S)
    nc = _NC_CACHE[S]
    maps = make_in_maps(inputs, S, B)
    res = run_bass_kernel_spmd(nc, maps, core_ids=list(range(2 * B)))
    out = np.empty((B, S, D), np.float32)
    for b in range(B):
        for c in range(2):
            o = res.results[2 * b + c]["out"]
            for j, t in enumerate(own_tiles(S, c)):
                out[b, t * TS:(t + 1) * TS] = o[j * TS:(j + 1) * TS]
    return out


def phase_a2(g):
    nc, S_ = g.nc, g.sch
    with contextlib.ExitStack() as s1:
        def sb(n, sh, dt):
            return s1.enter_context(nc.sbuf_tensor("a2_" + n, sh, dt))

        def ps(n, sh, dt):
            return s1.enter_context(nc.psum_tensor("a2_" + n, sh, dt))
        xr = Ring([(sb("xs%d" % i, [128, D], F32), Buf()) for i in range(3)])
        gbc = sb("gbc", [128, D], F32)
        hbr = Ring([(sb("hb%d" % i, [128, D], BF16), Buf()) for i in range(2)])
        junk = sb("junk", [128, D], BF16); junkB = Buf()
        hTr = Ring([(sb("hT%d" % i, [128, 16, 128], BF16), Buf()) for i in range(2)])
        wq = sb("wq", [128, 16, 768], BF16)
        uq = sb("uq", [128, 6, 1536], BF16)
        ql = sb("ql", [128, 768], F32); qlB = Buf()
        qlb = sb("qlb", [128, 768], BF16); qlbB = Buf()
        qlT = sb("qlT", [128, 6, 128], BF16); qlTB = Buf()
        qf = sb("qf", [128, 8, 192], F32); qfB = Buf()
        sqq = sb("sqq", [128, 8, 192], F32); sqqB = Buf()
        qtmp = sb("qtmp", [128, 8, 64], F32); qtmpB = Buf()
        qb = sb("qb", [128, 8, 192], BF16); qbB = Buf()
        QTr = Ring([(sb("QTst%d" % i, [128, 8, TS], BF16), Buf()) for i in range(2)])
        QPr = Ring([(sb("QPst%d" % i, [64, 8, TS], BF16), Buf()) for i in range(2)])
        csr = Ring([(sb("cs%d" % i, [128, 4, 64], F32), Buf()) for i in range(2)])
        gql = sb("gql", [128, 768], F32)
        gqh = sb("gqh", [128, 192], F32)
        identf = sb("identf", [128, 128], F32)
        identb = sb("identb", [128, 128], BF16)
        ssr = Ring([(sb("ss%d" % i, [128, 1], F32), Buf()) for i in range(4)])
        ssq = sb("ssq", [128, 8], F32); ssqB = Buf()
        pfs = Ring([(ps("pf%d" % i, [128, 512], F32), Buf()) for i in range(6)])
        pbs = Ring([(ps("pb%d" % i, [128, 1024], BF16), Buf()) for i in range(2)])
        cB = Buf()
        S_.dma("sp", DMA(gbc[:], g.attn_norm_g[0:1, :].partition_broadcast(128)), writes=[cB])
        S_.dma("sp", DMA(gql[:], g.q_lat_norm[0:1, :].partition_broadcast(128)), writes=[cB])
        S_.dma("sp", DMA(gqh[:], g.q_head_norm[0:1, :].partition_broadcast(128)), writes=[cB])
        S_.dma("sp", DMA(identf[:], g.ident[:, :]), writes=[cB])
        S_.dma("sp", DMA(wq[:].rearrange("p k c -> p (k c)"), g.wqb[:, :]), writes=[cB])
        S_.dma("sp", DMA(uq[:].rearrange("p k c -> p (k c)"), g.uqb[:, :]), writes=[cB])
        S_.op("dve", CPY(identb[:], identf[:]), reads=[cB], writes=[cB])
        S_.op("dve", TSC(gqh[:], gqh[:], float(QK ** -0.5), None, ALU.mult), reads=[cB], writes=[cB])
        S_.barrier()

        def load_x(n):
            xs, xsB = xr.next()
            S_.dma("sp", DMA(xs[:], g.x_own[n * 128:(n + 1) * 128, :]), writes=[xsB])
            return xs, xsB
        nsub = g.SO // 128
        pend = [load_x(0), load_x(1)]
        for n in range(nsub):
            T, s = n // 4, n % 4
            if s == 0:
                cs, csB = csr.next()
                S_.dma("sp", DMA(cs[:], g.cs_own[T * TS:(T + 1) * TS, :].rearrange("(s p) c -> p s c", p=128)), writes=[csB])
                QTst, QTB = QTr.next()
                QPst, QPB = QPr.next()
            xs, xsB = pend.pop(0)
            if n + 2 < nsub:
                pend.append(load_x(n + 2))
            hb, hbB = hbr.next()
            hT, hTB = hTr.next()
            ss, ssB = ssr.next()
            x_to_hT(g, S_, xs[:], xsB, gbc[:], hb[:], hbB, junk[:], junkB, ss[:], ssB, identb[:], pbs, hT, hTB, 0)
            for (c0, cb) in ((0, 512), (512, 256)):
                pf, pfB = pfs.next()
                for k in range(16):
                    S_.op("pe", MM(pf[:, 0:cb], hT[:, k, :], wq[:, k, c0:c0 + cb], k == 0, k == 15), reads=[hTB], writes=[pfB])
                S_.op("act", CPY(ql[:, c0:c0 + cb], pf[:, 0:cb]), reads=[pfB], writes=[qlB])
            ss, ssB = ssr.next()
            S_.op("act", ACTF(junk[:, 0:768], ql[:], AF.Square, accum_out=ss[:]), reads=[qlB], writes=[junkB, ssB])
            rstd_ops(S_, ss[:], QR, EPS, ssB)
            S_.op("dve", STT(qlb[:], ql[:], ss[:], gql[:], ALU.mult, ALU.mult), reads=[qlB, ssB], writes=[qlbB])
            pb, pbB = pbs.next()
            for k in range(6):
                S_.op("pe", TR(pb[:, k * 128:(k + 1) * 128], qlb[:, k * 128:(k + 1) * 128], identb[:]), reads=[qlbB], writes=[pbB])
            S_.op("act", CPY(qlT[:], pb[:, 0:768].rearrange("p (k t) -> p k t", t=128)), reads=[pbB], writes=[qlTB])
            for nb in range(4):
                pf, pfB = pfs.next()
                for k in range(6):
                    S_.op("pe", MM(pf[:, 0:384], qlT[:, k, :], uq[:, k, nb * 384:(nb + 1) * 384], k == 0, k == 5), reads=[qlTB], writes=[pfB])
                S_.op("act" if nb % 2 == 0 else "dve",
                      CPY(qf[:, 2 * nb:2 * nb + 2, :], pf[:, 0:384].rearrange("p (h c) -> p h c", c=192)), reads=[pfB], writes=[qfB])
            S_.op("dve", TT(sqq[:], qf[:], qf[:], ALU.mult), reads=[qfB], writes=[sqqB])
            S_.op("dve", RED(ssq[:], sqq[:]), reads=[sqqB], writes=[ssqB])
            rstd_ops(S_, ssq[:], QK, EPS, ssqB)
            S_.op("dve", TT(sqq[:], qf[:], bc_last(ssq[:], 192), ALU.mult), reads=[qfB, ssqB], writes=[sqqB])
            S_.op("dve", TT(qf[:], sqq[:], bc_mid(gqh[:], 8), ALU.mult), reads=[sqqB], writes=[qfB])
            S_.op("act", CPY(qb[:, :, 0:128], qf[:, :, 0:128]), reads=[qfB], writes=[qbB])
            rope_ops(S_, qf[:, :, 128:192], qfB, qtmp[:], qtmpB, qb[:, :, 128:192], qbB, cs[:, s, :], csB, 8)
            for half in range(2):
                pb, pbB = pbs.next()
                for h in range(8):
                    if half == 0:
                        S_.op("pe", TR(pb[:, h * 128:(h + 1) * 128], qb[:, h, 0:128], identb[:]), reads=[qbB], writes=[pbB])
                    else:
                        S_.op("pe", TR(pb[0:64, h * 128:(h + 1) * 128], qb[:, h, 128:192], identb[:]), reads=[qbB], writes=[pbB])
                if half == 0:
                    S_.op("act", CPY(QTst[:, :, s * 128:(s + 1) * 128], pb[:].rearrange("p (h t) -> p h t", t=128)), reads=[pbB], writes=[QTB])
                else:
                    S_.op("dve", CPY(QPst[:, :, s * 128:(s + 1) * 128], pb[0:64, :].rearrange("p (h t) -> p h t", t=128)), reads=[pbB], writes=[QPB])
            if s == 3:
                c0t = T * TS
                S_.dma("sp", DMA(g.QT[:, :, c0t:c0t + TS].rearrange("h d t -> d h t"), QTst[:]), reads=[QTB], writes=[Buf()])
                S_.dma("sp", DMA(g.QPT[:, :, c0t:c0t + TS].rearrange("h d t -> d h t"), QPst[:]), reads=[QPB], writes=[Buf()])
        S_.barrier()
        S_.emit()


def phase_c(g):
    nc, S_ = g.nc, g.sch
    S, SO, NSL = g.S, g.SO, g.NSL
    with contextlib.ExitStack() as s1:
        def sb(n, sh, dt):
            return s1.enter_context(nc.sbuf_tensor("c_" + n, sh, dt))

        def ps(n, sh, dt):
            return s1.enter_context(nc.psum_tensor("c_" + n, sh, dt))
        hr = Ring([((sb("KTh%d" % i, [128, S], BF16), sb("Vh%d" % i, [128, S // 128, 128], BF16),
                     sb("QTh%d" % i, [128, SO], BF16), sb("QPh%d" % i, [64, SO], BF16)), Buf()) for i in range(2)])
        KP = sb("KP", [64, S], BF16)
        dmf = sb("dmf", [128, 2048], F32)
        dm = sb("dm", [128, 2048], BF16)
        mA = sb("mA", [128, 2048], BF16); mAB = Buf()
        mB = sb("mB", [128, 2048], BF16); mBB = Buf()
        ones = sb("ones", [128, 128], BF16)
        selbc = sb("selbc", [128, 2 * NSL], F32)
        ptr = Ring([(sb("pt%d" % i, [128, 512], BF16), Buf()) for i in range(4)])
        recr = Ring([(sb("rec%d" % i, [128, 512], F32), Buf()) for i in range(2)])
        aor = Ring([(sb("ao%d" % i, [128, 512], BF16), Buf()) for i in range(2)])
        stp = Ring([(ps("st%d" % i, [128, 512], F32), Buf()) for i in range(3)])
        opr = Ring([(ps("op%d" % i, [128, 512], F32), Buf()) for i in range(2)])
        dnr = Ring([(ps("dn%d" % i, [128, 512], F32), Buf()) for i in range(2)])
        cB = Buf()
        S_.dma("sp", DMA(KP[:], g.KPT[:, :]), writes=[cB])
        S_.dma("sp", DMA(dmf[:], g.diagmask[:, :]), writes=[cB])
        S_.dma("sp", DMA(selbc[:], g.sel[0:1, :].partition_broadcast(128)), writes=[cB])
        S_.op("dve", CPY(dm[:], dmf[:]), reads=[cB], writes=[cB])
        S_.op("dve", MEMSET(ones[:], 1.0), writes=[cB])
        S_.barrier()

        def load_head(h):
            (KTh, Vh, QTh, QPh), hB = hr.next()
            S_.dma("sp", DMA(KTh[:], g.KT[h]), writes=[hB])
            S_.dma("sp", DMA(Vh[:], g.V[:, h * 128:(h + 1) * 128].rearrange("(n p) d -> p n d", p=128)), writes=[hB])
            S_.dma("sp", DMA(QTh[:], g.QT[h]), writes=[hB])
            S_.dma("sp", DMA(QPh[:], g.QPT[h]), writes=[hB])
            return (KTh, Vh, QTh, QPh), hB
        nxt = load_head(0)
        for h in range(NH):
            (KTh, Vh, QTh, QPh), hB = nxt
            if h + 1 < NH:
                nxt = load_head(h + 1)
            for j in range(NSL):
                nkb = 4 * (2 * j + 2)
                qc = slice(j * TS, (j + 1) * TS)
                S_.op("pool", TSC(mA[:], dm[:], selbc[:, j:j + 1], selbc[:, NSL + j:NSL + j + 1], ALU.mult, ALU.add), writes=[mAB])
                S_.op("pool", TSC(mB[:], dm[:], selbc[:, NSL + j:NSL + j + 1], None, ALU.mult), writes=[mBB])
                O, OB = opr.next()
                dn, dnB = dnr.next()
                sts = {}
                pts = {}

                def qk(kb):
                    st, stB = stp.next()
                    kc = slice(kb * 128, (kb + 1) * 128)
                    S_.op("pe", MM(st[:, :], KTh[:, kc], QTh[:, qc], True, False), reads=[hB], writes=[stB])
                    S_.op("pe", MM(st[:, :], KP[:, kc], QPh[:, qc], False, True), reads=[hB], writes=[stB])
                    sts[kb] = (st, stB)

                def ex(kb):
                    st, stB = sts.pop(kb)
                    pt, ptB = ptr.next()
                    S_.op("act", ACTF(pt[:], st[:, :], AF.Exp), reads=[stB], writes=[ptB])
                    u = kb // 4
                    if u == 2 * j:
                        S_.op("dve", TT(pt[:], pt[:], mA[:, (kb % 4) * 512:(kb % 4 + 1) * 512], ALU.mult), reads=[ptB, mAB], writes=[ptB])
                    elif u == 2 * j + 1:
                        S_.op("dve", TT(pt[:], pt[:], mB[:, (kb % 4) * 512:(kb % 4 + 1) * 512], ALU.mult), reads=[ptB, mBB], writes=[ptB])
                    pts[kb] = (pt, ptB)

                def pv(kb):
                    pt, ptB = pts.pop(kb)
                    S_.op("pe", MM(O[:, :], Vh[:, kb, :], pt[:], kb == 0, kb == nkb - 1), reads=[hB, ptB], writes=[OB])
                    S_.op("pe", MM(dn[:, :], ones[:], pt[:], kb == 0, kb == nkb - 1), reads=[ptB], writes=[dnB])
                qk(0)
                qk(1)
                for kb in range(nkb):
                    ex(kb)
                    if kb + 2 < nkb:
                        qk(kb + 2)
                    pv(kb)
                rec, recB = recr.next()
                ao, aoB = aor.next()
                S_.op("dve", RCP(rec[:], dn[:, :]), reads=[dnB], writes=[recB])
                S_.op("dve", TT(ao[:], O[:, :], rec[:], ALU.mult), reads=[OB, recB], writes=[aoB])
                S_.dma("sp", DMA(g.AT[h * 128:(h + 1) * 128, qc], ao[:]), reads=[aoB], writes=[Buf()])
        S_.barrier()
        S_.emit()


def phase_e(g):
    nc, S_ = g.nc, g.sch
    NSL = g.NSL
    with contextlib.ExitStack() as s1:
        def sb(n, sh, dt):
            return s1.enter_context(nc.sbuf_tensor("e_" + n, sh, dt))

        def ps(n, sh, dt):
            return s1.enter_context(nc.psum_tensor("e_" + n, sh, dt))
        actT = sb("actT", [128, 16, TS], BF16); actB = Buf()
        btmp = sb("btmp", [128, 8, TS], BF16); btB = Buf()
        x1 = sb("x1", [128, 4, D], F32); x1B = [Buf() for _ in range(4)]
        hidT = sb("hidT", [128, 44, TS], BF16); hidB = Buf()
        wor = Ring([(sb("wo%d" % i, [128, 16, 256], BF16), Buf()) for i in range(2)])
        wgr = Ring([(sb("wgu%d" % i, [128, 2, 16, 128], BF16), Buf()) for i in range(4)])
        wdr = Ring([(sb("wd%d" % i, [128, 11, 512], BF16), Buf()) for i in range(3)])
        obr = Ring([(sb("ob%d" % i, [128, 512], F32), Buf()) for i in range(3)])
        sgr = Ring([(sb("sg%d" % i, [128, 512], BF16), Buf()) for i in range(2)])
        gff = sb("gff", [128, D], F32)
        hbr = Ring([(sb("hb%d" % i, [128, D], BF16), Buf()) for i in range(2)])
        identf = sb("identf", [128, 128], F32)
        identb = sb("identb", [128, 128], BF16)
        selbc = sb("selbc", [128, 2 * NSL], F32)
        ssr = Ring([(sb("ss%d" % i, [128, 1], F32), Buf()) for i in range(4)])
        pfs = Ring([(ps("pf%d" % i, [128, 512], F32), Buf()) for i in range(6)])
        pbs = Ring([(ps("pb%d" % i, [128, 1024], BF16), Buf()) for i in range(2)])
        cB = Buf()
        S_.dma("sp", DMA(gff[:], g.ffn_norm_g[0:1, :].partition_broadcast(128)), writes=[cB])
        S_.dma("sp", DMA(identf[:], g.ident[:, :]), writes=[cB])
        S_.dma("sp", DMA(selbc[:], g.sel[0:1, :].partition_broadcast(128)), writes=[cB])
        S_.op("dve", CPY(identb[:], identf[:]), reads=[cB], writes=[cB])
        S_.barrier()
        def mk_wo(b):
            def f():
                w, wB = wor.next()
                S_.dma("sp", DMA(w[:].rearrange("p k c -> p (k c)"), g.wbO[b]), writes=[wB])
                return w, wB
            return f

        def mk_wg(hc):
            def f():
                w, wB = wgr.next()
                S_.dma("sp", DMA(w[:].rearrange("p a k c -> p (a k c)"), g.wbGU[hc]), writes=[wB])
                return w, wB
            return f

        def mk_wd(i):
            def f():
                w, wB = wdr.next()
                S_.dma("sp", DMA(w[:].rearrange("p k c -> p (k c)"), g.wbD[i]), writes=[wB])
                return w, wB
            return f
        for j in range(NSL):
            wl = [mk_wo(b) for b in range(8)] + [mk_wg(hc) for hc in range(44)] + [mk_wd(i) for i in range(16)]
            issued = {}
            nissued = [0]

            def getw(i):
                for ii in range(i, i + 4):
                    depth = 3 if 8 <= ii < 52 else (2 if ii >= 52 else 1)
                    if ii >= len(wl):
                        break
                    if ii < nissued[0]:
                        continue
                    if (ii - i) > depth:
                        break
                    issued[ii] = wl[ii]()
                    nissued[0] = ii + 1
                return issued.pop(i)
            qc = slice(j * TS, (j + 1) * TS)
            S_.dma("sp", DMA(actT[:, 0:8, :], g.AT[:, qc].rearrange("(k p) t -> p k t", p=128)), writes=[actB])
            S_.dma("sp", DMA(actT[:, 8:16, :], g.BT[:, (2 * j) * TS:(2 * j + 1) * TS].rearrange("(k p) t -> p k t", p=128)), writes=[actB])
            S_.dma("sp", DMA(btmp[:], g.BT[:, (2 * j + 1) * TS:(2 * j + 2) * TS].rearrange("(k p) t -> p k t", p=128)), writes=[btB])
            for s in range(4):
                S_.dma("sp", DMA(x1[:, s, :], g.x_own[j * TS + s * 128:j * TS + (s + 1) * 128, :]), writes=[x1B[s]])
            S_.op("pool", TSC(btmp[:], btmp[:], selbc[:, NSL + j:NSL + j + 1], None, ALU.mult), reads=[btB], writes=[btB])
            S_.op("dve", STT(actT[:, 8:16, :], actT[:, 8:16, :], selbc[:, j:j + 1], btmp[:], ALU.mult, ALU.add), reads=[actB, btB], writes=[actB])
            for ob in range(8):
                w, wB = getw(ob)
                for s in range(4):
                    pf, pfB = pfs.next()
                    for k in range(16):
                        S_.op("pe", MM(pf[:, 0:256], actT[:, k, s * 128:(s + 1) * 128], w[:, k, :], k == 0, k == 15), reads=[actB, wB], writes=[pfB])
                    xs = x1[:, s, ob * 256:(ob + 1) * 256]
                    S_.op("dve", TT(xs, xs, pf[:, 0:256], ALU.add), reads=[pfB, x1B[s]], writes=[x1B[s]])
            for s in range(4):
                hb, hbB = hbr.next()
                ss, ssB = ssr.next()
                x_to_hT(g, S_, x1[:, s, :], x1B[s], gff[:], hb[:], hbB, hb[:], hbB, ss[:], ssB, identb[:], pbs, actT, actB, s * 128)
            for hc in range(44):
                w, wB = getw(8 + hc)
                pg, pgB = pfs.next()
                pu, puB = pfs.next()
                for k in range(16):
                    S_.op("pe", MM(pg[:, :], w[:, 0, k, :], actT[:, k, :], k == 0, k == 15), reads=[actB, wB], writes=[pgB])
                for k in range(16):
                    S_.op("pe", MM(pu[:, :], w[:, 1, k, :], actT[:, k, :], k == 0, k == 15), reads=[actB, wB], writes=[puB])
                sg, sgB = sgr.next()
                S_.op("act", ACTF(sg[:], pg[:, :], AF.Silu), reads=[pgB], writes=[sgB])
                S_.op("dve", TT(hidT[:, hc, :], pu[:, :], sg[:], ALU.mult), reads=[puB, sgB], writes=[hidB])
            for fb in range(4):
                banks = [pfs.next() for _ in range(4)]
                for hg in range(4):
                    w, wB = getw(52 + fb * 4 + hg)
                    for s in range(4):
                        pf, pfB = banks[s]
                        for k in range(11):
                            S_.op("pe", MM(pf[:, :], hidT[:, hg * 11 + k, s * 128:(s + 1) * 128], w[:, k, :],
                                           hg == 0 and k == 0, hg == 3 and k == 10), reads=[hidB, wB], writes=[pfB])
                for s in range(4):
                    pf, pfB = banks[s]
                    o, oB = obr.next()
                    S_.op("dve", TT(o[:], pf[:, :], x1[:, s, fb * 512:(fb + 1) * 512], ALU.add), reads=[pfB, x1B[s]], writes=[oB])
                    r0 = j * TS + s * 128
                    S_.dma("sp", DMA(g.out[r0:r0 + 128, fb * 512:(fb + 1) * 512], o[:]), reads=[oB], writes=[Buf()])
        S_.barrier()
        S_.emit()


def phase_d(g):
    nc, S_ = g.nc, g.sch
    NCH = g.NCH
    with contextlib.ExitStack() as s1:
        def sb(n, sh, dt):
            return s1.enter_context(nc.sbuf_tensor("d_" + n, sh, dt))

        def ps(n, sh, dt):
            return s1.enter_context(nc.psum_tensor("d_" + n, sh, dt))
        mixbc = sb("mixbc", [128, RIN], F32)
        cst = {}
        for nm, src in (("kk", g.k_k), ("ka", g.k_a), ("rk", g.r_k), ("w0", g.w0), ("a0", g.a0), ("lng", g.ln_g), ("lnb", g.ln_b)):
            cst[nm] = sb("c_" + nm, [128, RW], F32)
        w2b = sb("w2b", [64, RW], BF16); a2b = sb("a2b", [64, RW], BF16)
        g2a = sb("g2a", [128, RW], BF16); g2c = sb("g2c", [32, RW], BF16)
        tri = sb("tri", [128, 385], F32)
        trib = sb("trib", [128, 385], BF16)
        onesb = sb("onesb", [128, 16], BF16)
        sgh = sb("sgh", [128, RW], BF16); sghB = Buf()
        sgm = sb("sgm", [128, RW], BF16); sgmB = Buf()
        mar = sb("mar", [128, 256], F32); mlo = sb("mlo", [128, 128], F32)
        identf = sb("identf", [128, 128], F32); identb = sb("identb", [128, 128], BF16)
        curr = Ring([(sb("cur%d" % i, [128, RIN], F32), Buf()) for i in range(2)])
        prv = sb("prv", [128, RIN], F32); prvB = Buf()
        F = [sb("F%d" % i, [128, RW], F32) for i in range(7)]
        FB = [Buf() for _ in range(7)]
        lin = sb("lin", [128, 288], BF16); linB = Buf()
        linT = sb("linT", [128, 512], BF16); linTB = Buf()
        gt = sb("gt", [128, RW], BF16); gtB = Buf()
        tk = {}
        tkB = {}
        for nm in ("rt", "at", "bt", "kt", "bh", "kh", "vb"):
            tk[nm] = sb("t_" + nm, [128, RW], BF16); tkB[nm] = Buf()
        AR = sb("AR", [128, 8, 2, 2, 128], BF16); ARB = Buf()
        BTf = sb("BTf", [128, 8, 128], BF16); BTfB = Buf()
        KTf = sb("KTf", [128, 8, 128], BF16); KTfB = Buf()
        MakT = sb("MakT", [128, 16, 128], BF16); MakB = Buf()
        MrbT = sb("MrbT", [128, 16, 128], BF16); MrbB = Buf()
        MrkT = sb("MrkT", [128, 16, 128], BF16); MrkB = Buf()
        G7 = sb("G7", [128, 16, 128], BF16); G7B = Buf()
        Nm = sb("Nm", [128, 8, 128], BF16); NmB = [Buf() for _ in range(4)]
        NT = sb("NT", [128, 8, 128], BF16); NTB = [Buf() for _ in range(4)]
        GQ = [sb("GQ%d" % i, [128, 8, 2, 128], BF16) for i in range(2)]; GQB = [[Buf() for _ in range(4)] for _ in range(2)]
        QTt = [sb("QTt%d" % i, [128, 8, 128], BF16) for i in range(2)]; QTB = [[Buf() for _ in range(2)] for _ in range(2)]
        W1b = sb("W1b", [128, RW], BF16); W1B = [Buf(), Buf()]
        Ub = sb("Ub", [128, RW], BF16); UbB = [Buf(), Buf()]
        H = sb("H", [128, 8, 64], F32); HB = Buf()
        Hb = sb("Hb", [128, 8, 64], BF16); HbB = Buf()
        PC = sb("PC", [128, 8], F32); PCB = Buf()
        ssk = sb("ssk", [128, 16], F32); sskB = Buf()
        bon = sb("bon", [128, 16], F32); bonB = Buf()
        mu = sb("mu", [128, 16], F32); muB = Buf()
        var = sb("var", [128, 16], F32); varB = Buf()
        obf = sb("obf", [128, RW], BF16); obfB = Buf()
        BTr = Ring([(sb("BTs%d" % i, [128, 8, 128], BF16), Buf()) for i in range(1)])
        pfs = Ring([(ps("pf%d" % i, [128, 512], F32), Buf()) for i in range(6)])
        pbs = Ring([(ps("pb%d" % i, [128, 1024], BF16), Buf()) for i in range(2)])
        cB = Buf()
        S_.dma("sp", DMA(mixbc[:], g.shift_mix[0:1, :].partition_broadcast(128)), writes=[cB])
        for nm, src in (("kk", g.k_k), ("ka", g.k_a), ("rk", g.r_k), ("w0", g.w0), ("a0", g.a0), ("lng", g.ln_g), ("lnb", g.ln_b)):
            S_.dma("sp", DMA(cst[nm][:], src[0:1, :].partition_broadcast(128)), writes=[cB])
        S_.dma("sp", DMA(w2b[:], g.w2b[:, :]), writes=[cB])
        S_.dma("sp", DMA(a2b[:], g.a2b[:, :]), writes=[cB])
        S_.dma("sp", DMA(g2a[:], g.g2b[0:128, :]), writes=[cB])
        S_.dma("sp", DMA(g2c[:], g.g2b[128:160, :]), writes=[cB])
        S_.dma("sp", DMA(tri[:], g.tri[:, :]), writes=[cB])
        S_.dma("sp", DMA(mar[:], g.rmask_ar[:, :]), writes=[cB])
        S_.dma("sp", DMA(mlo[:], g.rmask_lo[:, :]), writes=[cB])
        S_.dma("sp", DMA(identf[:], g.ident[:, :]), writes=[cB])
        S_.op("dve", CPY(identb[:], identf[:]), reads=[cB], writes=[cB])
        S_.op("dve", CPY(trib[:], tri[:]), reads=[cB], writes=[cB])
        S_.op("dve", MEMSET(onesb[:], 1.0), writes=[cB])
        S_.op("dve", MEMSET(AR[:], 0.0), writes=[ARB])
        S_.op("dve", MEMSET(H[:], 0.0), writes=[HB])
        S_.op("dve", MEMSET(Hb[:], 0.0), writes=[HbB])
        S_.barrier()
        h3 = lambda ap: ap.rearrange("p (h c) -> p h c", c=64)

        def load_cur(c):
            cur, curB = curr.next()
            S_.dma("sp", DMA(cur[:], g.PR[c * 128:(c + 1) * 128, :]), writes=[curB])
            return cur, curB

        def load_prv(c):
            if c == 0:
                S_.op("dve", MEMSET(prv[0:1, :], 0.0), writes=[prvB])
                S_.dma("sp", DMA(prv[1:128, :], g.PR[0:127, :]), writes=[prvB])
            else:
                S_.dma("sp", DMA(prv[:], g.PR[c * 128 - 1:c * 128 + 127, :]), writes=[prvB])
        import os
        gtr = Ring([(gt, gtB), (sb("gt1", [128, RW], BF16), Buf())])
        Y8 = sb("Y8", [128, RW], F32); Y8B = Buf()
        NCHL = int(os.environ.get('DCH', str(NCH)))

        class Cx:
            pass

        def st12(cx):
            cur, curB = cx.cur, cx.curB
            if True:
                S_.op("dve", TT(prv[:], prv[:], cur[:], ALU.subtract), reads=[prvB, curB], writes=[prvB])
                yield
                S_.op("dve", TT(prv[:], prv[:], mixbc[:], ALU.mult), reads=[prvB], writes=[prvB])
                yield
                S_.op("dve", TT(cur[:], cur[:], prv[:], ALU.add), reads=[prvB, curB], writes=[curB])
                yield
                if cx.c + 1 < NCHL:
                    load_prv(cx.c + 1)
                cx.r_, cx.k_, cx.v_ = cur[:, 0:1024], cur[:, 1024:2048], cur[:, 2048:3072]
                gt, gtB = gtr.next()
                cx.gt, cx.gtB = gt, gtB
                S_.op("act", ACTF(lin[:, 0:64], cur[:, 3072:3136], AF.Tanh), reads=[curB], writes=[linB])
                yield
                S_.op("act", ACTF(lin[:, 128:288], cur[:, 3200:3360], AF.Sigmoid), reads=[curB], writes=[linB])
                yield
                S_.op("dve", CPY(lin[:, 64:128], cur[:, 3136:3200]), reads=[curB], writes=[linB])
                yield
                pb, pbB = pbs.next()
                S_.op("pe", TR(pb[0:64, 0:128], lin[:, 0:64], identb[:]), reads=[linB], writes=[pbB])
                yield
                S_.op("pe", TR(pb[0:64, 128:256], lin[:, 64:128], identb[:]), reads=[linB], writes=[pbB])
                yield
                S_.op("pe", TR(pb[:, 256:384], lin[:, 128:256], identb[:]), reads=[linB], writes=[pbB])
                yield
                S_.op("pe", TR(pb[0:32, 384:512], lin[:, 256:288], identb[:]), reads=[linB], writes=[pbB])
                yield
                S_.op("act", CPY(linT[0:64, 0:256], pb[0:64, 0:256]), reads=[pbB], writes=[linTB])
                yield
                S_.op("act", CPY(linT[:, 256:384], pb[:, 256:384]), reads=[pbB], writes=[linTB])
                yield
                S_.op("act", CPY(linT[0:32, 384:512], pb[0:32, 384:512]), reads=[pbB], writes=[linTB])
                yield
                for n in range(2):
                    cs_ = slice(n * 512, (n + 1) * 512)
                    pf, pfB = pfs.next()
                    S_.op("pe", MM(pf[:, :], linT[0:64, 0:128], w2b[:, cs_]), reads=[linTB], writes=[pfB])
                    S_.op("dve", TT(F[0][:, cs_], pf[:, :], cst["w0"][:, cs_], ALU.add), reads=[pfB], writes=[FB[0]])
                    pf, pfB = pfs.next()
                    S_.op("pe", MM(pf[:, :], linT[0:64, 128:256], a2b[:, cs_]), reads=[linTB], writes=[pfB])
                    S_.op("dve", TT(F[1][:, cs_], pf[:, :], cst["a0"][:, cs_], ALU.add), reads=[pfB], writes=[FB[1]])
                    pf, pfB = pfs.next()
                    S_.op("pe", MM(pf[:, :], linT[:, 256:384], g2a[:, cs_], True, False), reads=[linTB], writes=[pfB])
                    S_.op("pe", MM(pf[:, :], linT[0:32, 384:512], g2c[:, cs_], False, True), reads=[linTB], writes=[pfB])
                    S_.op("act", CPY(gt[:, cs_], pf[:, :]), reads=[pfB], writes=[gtB])
                S_.op("act", ACTF(F[0][:], F[0][:], AF.Sigmoid), reads=[FB[0]], writes=[FB[0]])
                yield
                S_.op("act", ACTF(F[1][:], F[1][:], AF.Sigmoid), reads=[FB[1]], writes=[FB[1]])
                yield

            yield

        def st78(cx):
            if True:
                def hsl(h):
                    return slice(h * 64, (h + 1) * 64)

                def rows_of(h):
                    return slice((h % 2) * 64, (h % 2 + 1) * 64)
                w1p = [pfs.next(), pfs.next()]
                for h in range(16):
                    pf, pfB = w1p[h // 8]
                    o_ = slice((h % 8) * 64, (h % 8 + 1) * 64)
                    S_.op("pe", MM(pf[:, o_], AR[:, h // 2, h % 2, 0, :], Hb[:, h // 2, :], True, False), reads=[ARB, HbB], writes=[pfB])
                    S_.op("pe", MM(pf[:, o_], MakT[:, h, :], tk["vb"][:, hsl(h)], False, True), reads=[MakB, tkB["vb"]], writes=[pfB])
                S_.op("act", CPY(W1b[:, 0:512], w1p[0][0][:, :]), reads=[w1p[0][1]], writes=[W1B[0]])
                yield
                S_.op("dve", CPY(W1b[:, 512:1024], w1p[1][0][:, :]), reads=[w1p[1][1]], writes=[W1B[1]])
                yield
                up_ = [pfs.next(), pfs.next()]
                for h in range(16):
                    pf, pfB = up_[h // 8]
                    o_ = slice((h % 8) * 64, (h % 8 + 1) * 64)
                    S_.op("pe", MM(pf[:, o_], G7[:, h, :], W1b[:, hsl(h)]), reads=[G7B, W1B[h // 8]], writes=[pfB])
                S_.op("act", CPY(Ub[:, 0:512], up_[0][0][:, :]), reads=[up_[0][1]], writes=[UbB[0]])
                yield
                S_.op("dve", CPY(Ub[:, 512:1024], up_[1][0][:, :]), reads=[up_[1][1]], writes=[UbB[1]])
                yield
                yp = [pfs.next(), pfs.next()]
                for h in range(16):
                    pf, pfB = yp[h // 8]
                    o_ = slice((h % 8) * 64, (h % 8 + 1) * 64)
                    S_.op("pe", MM(pf[:, o_], AR[:, h // 2, h % 2, 1, :], Hb[:, h // 2, :], True, False), reads=[ARB, HbB], writes=[pfB])
                    S_.op("pe", MM(pf[:, o_], MrbT[:, h, :], Ub[:, hsl(h)], False, False), reads=[MrbB, UbB[h // 8]], writes=[pfB])
                    S_.op("pe", MM(pf[:, o_], MrkT[:, h, :], tk["vb"][:, hsl(h)], False, True), reads=[MrkB, tkB["vb"]], writes=[pfB])
                hn = [pfs.next(), pfs.next()]
                for h in range(16):
                    pf, pfB = hn[h // 8]
                    o_ = slice((h % 8) * 64, (h % 8 + 1) * 64)
                    pc_ = slice((h // 2) * 128, (h // 2 + 1) * 128)
                    S_.op("pe", MM(pf[:, o_], tk["bh"][:, pc_], Ub[:, hsl(h)], True, False), reads=[tkB["bh"], UbB[h // 8]], writes=[pfB])
                    S_.op("pe", MM(pf[:, o_], tk["kh"][:, pc_], tk["vb"][:, hsl(h)], False, True), reads=[tkB["kh"], tkB["vb"]], writes=[pfB])
                S_.op("act", CPY(Y8[:, 0:512], yp[0][0][:, :]), reads=[yp[0][1], Y8B], writes=[Y8B])
                yield
                S_.op("act", CPY(Y8[:, 512:1024], yp[1][0][:, :]), reads=[yp[1][1], Y8B], writes=[Y8B])
                yield
                for half in range(2):
                    rows = slice(half * 64, (half + 1) * 64)
                    S_.op("dve", TT(H[rows, :, :], H[rows, :, :], bc_last(PC[rows, :], 64), ALU.mult), reads=[HB, PCB, HbB], writes=[HB])
                    for q in range(2):
                        src = hn[q][0][rows, :].rearrange("p (a b c) -> p a b c", b=2, c=64)[:, :, half, :]
                        S_.op("dve", TT(H[rows, 4 * q:4 * q + 4, :], H[rows, 4 * q:4 * q + 4, :], src, ALU.add), reads=[HB, hn[q][1]], writes=[HB])
                S_.op("act", CPY(Hb[:], H[:]), reads=[HB], writes=[HbB])
                yield
                y3 = h3(Y8[:])
                S_.op("dve", RED(mu[:], y3), reads=[Y8B], writes=[muB])
                yield
                S_.op("dve", TSC(mu[:], mu[:], 1.0 / 64, None, ALU.mult), reads=[muB], writes=[muB])
                yield
                S_.op("dve", TT(h3(F[5][:]), y3, bc_last(mu[:], 64), ALU.subtract), reads=[Y8B, muB, FB[5]], writes=[FB[5]])
                yield
                S_.op("pool", TT(F[6][:], F[5][:], F[5][:], ALU.mult), reads=[FB[5], FB[6]], writes=[FB[6]])
                yield
                S_.op("dve", RED(var[:], h3(F[6][:])), reads=[FB[6]], writes=[varB])
                yield
                rstd_ops(S_, var[:], 64, GN_EPS, varB)
                yield
                S_.op("dve", TT(h3(F[5][:]), h3(F[5][:]), bc_last(var[:], 64), ALU.mult), reads=[FB[5], varB], writes=[FB[5]])
                yield
                S_.op("pool", TT(F[5][:], F[5][:], cst["lng"][:], ALU.mult), reads=[FB[5]], writes=[FB[5]])
                yield
                S_.op("pool", TT(F[5][:], F[5][:], cst["lnb"][:], ALU.add), reads=[FB[5]], writes=[FB[5]])
                yield
                S_.op("dve", TT(h3(F[6][:]), h3(cx.v_), bc_last(bon[:], 64), ALU.mult), reads=[cx.curB, bonB, varB], writes=[FB[6]])
                yield
                S_.op("dve", TT(F[5][:], F[5][:], F[6][:], ALU.add), reads=[FB[5], FB[6]], writes=[FB[5]])
                yield
                S_.op("dve", TT(obf[:], F[5][:], cx.gt[:], ALU.mult), reads=[FB[5], cx.gtB], writes=[obfB])
                yield
                pb, pbB = pbs.next()
                for k in range(8):
                    S_.op("pe", TR(pb[:, k * 128:(k + 1) * 128], obf[:, k * 128:(k + 1) * 128], identb[:]), reads=[obfB], writes=[pbB])
                BTs, BTsB = BTr.next()
                S_.op("act", CPY(BTs[:], pb[:].rearrange("p (k t) -> p k t", t=128)), reads=[pbB], writes=[BTsB])
                yield
                S_.dma("sp", DMA(g.BT[:, cx.c * 128:(cx.c + 1) * 128].rearrange("(k p) t -> p k t", p=128), BTs[:]), reads=[BTsB], writes=[Buf()])
                yield

            yield

        def run_all(gen):
            for _ in gen:
                pass

        def interleave(g1, g2):
            done1 = done2 = False
            while not (done1 and done2):
                if not done1:
                    try:
                        next(g1)
                    except StopIteration:
                        done1 = True
                if not done2:
                    try:
                        next(g2)
                    except StopIteration:
                        done2 = True
        cxs = [Cx() for _ in range(NCHL)]
        for c_, cx in enumerate(cxs):
            cx.c = c_
        cxs[0].cur, cxs[0].curB = load_cur(0)
        load_prv(0)
        run_all(st12(cxs[0]))
        for c in range(NCHL):
            cx = cxs[c]
            cur, curB = cx.cur, cx.curB
            r_, k_, v_ = cx.r_, cx.k_, cx.v_
            if c + 1 < NCHL:
                cxs[c + 1].cur, cxs[c + 1].curB = load_cur(c + 1)
            S_.op("dve", TT(F[2][:], k_, cst["kk"][:], ALU.mult), reads=[curB], writes=[FB[2]])
            S_.op("dve", TT(F[5][:], F[2][:], F[2][:], ALU.mult), reads=[FB[2]], writes=[FB[5]])
            S_.op("dve", RED(ssk[:], h3(F[5][:])), reads=[FB[5]], writes=[sskB])
            S_.op("dve", TSC(ssk[:], ssk[:], 1e-24, None, ALU.max), reads=[sskB], writes=[sskB])
            S_.op("act", ACTF(ssk[:], ssk[:], AF.Sqrt), reads=[sskB], writes=[sskB])
            S_.op("dve", RCP(ssk[:], ssk[:]), reads=[sskB], writes=[sskB])
            S_.op("dve", TT(h3(F[2][:]), h3(F[2][:]), bc_last(ssk[:], 64), ALU.mult), reads=[FB[2], sskB], writes=[FB[2]])
            S_.op("dve", TT(F[3][:], F[2][:], F[1][:], ALU.mult), reads=[FB[2], FB[1]], writes=[FB[3]])
            S_.op("dve", STT(F[4][:], F[1][:], -1.0, cst["ka"][:], ALU.add, ALU.mult), reads=[FB[1]], writes=[FB[4]])
            S_.op("dve", STT(F[4][:], F[4][:], 1.0, k_, ALU.add, ALU.mult), reads=[FB[4], curB], writes=[FB[4]])
            S_.op("dve", TT(F[5][:], r_, F[4][:], ALU.mult), reads=[curB, FB[4], sskB], writes=[FB[5]])
            S_.op("dve", TT(F[5][:], F[5][:], cst["rk"][:], ALU.mult), reads=[FB[5]], writes=[FB[5]])
            S_.op("dve", RED(bon[:], h3(F[5][:])), reads=[FB[5]], writes=[bonB])
            S_.op("act", CPY(tk["vb"][:], v_), reads=[curB], writes=[tkB["vb"]])
            S_.op("act", CPY(sgh[:], F[0][:]), reads=[FB[0]], writes=[sghB])
            S_.op("dve", TT(F[6][:], F[0][:], sgh[:], ALU.subtract), reads=[FB[0], sghB, FB[6]], writes=[FB[6]])
            S_.op("act", CPY(sgm[:], F[6][:]), reads=[FB[6]], writes=[sgmB])

            def cums(which, scale, dstF):
                for n in range(2):
                    cs_ = slice(n * 512, (n + 1) * 512)
                    pf, pfB = pfs.next()
                    S_.op("pe", MM(pf[:, :], trib[:, which * 128:(which + 1) * 128], sgh[:, cs_], True, False), reads=[sghB], writes=[pfB])
                    S_.op("pe", MM(pf[:, :], trib[:, which * 128:(which + 1) * 128], sgm[:, cs_], False, True), reads=[sgmB], writes=[pfB])
                    S_.op("act", ACTF(F[dstF][:, cs_], pf[:, :], AF.Exp, scale=scale * CDEC), reads=[pfB, FB[dstF]], writes=[FB[dstF]])
            cums(0, 1.0, 5)
            S_.op("dve", TT(tk["rt"][:], r_, F[5][:], ALU.mult), reads=[curB, FB[5]], writes=[tkB["rt"]])
            cums(0, -1.0, 6)
            S_.op("dve", TT(tk["bt"][:], F[3][:], F[6][:], ALU.mult), reads=[FB[3], FB[6]], writes=[tkB["bt"]])
            S_.op("dve", TT(tk["kt"][:], F[4][:], F[6][:], ALU.mult), reads=[FB[4], FB[6]], writes=[tkB["kt"]])
            cums(1, 1.0, 5)
            S_.op("dve", STT(tk["at"][:], F[2][:], -1.0, F[5][:], ALU.mult, ALU.mult), reads=[FB[2], FB[5]], writes=[tkB["at"]])
            cums(2, 1.0, 6)
            S_.op("dve", TT(tk["bh"][:], F[3][:], F[6][:], ALU.mult), reads=[FB[3], FB[6]], writes=[tkB["bh"]])
            S_.op("dve", TT(tk["kh"][:], F[4][:], F[6][:], ALU.mult), reads=[FB[4], FB[6]], writes=[tkB["kh"]])
            pf, pfB = pfs.next()
            for hp in range(8):
                S_.op("pe", MM(pf[:, hp * 16:(hp + 1) * 16], sgh[:, hp * 128:(hp + 1) * 128], onesb[:], True, False), reads=[sghB], writes=[pfB])
                S_.op("pe", MM(pf[:, hp * 16:(hp + 1) * 16], sgm[:, hp * 128:(hp + 1) * 128], onesb[:], False, True), reads=[sgmB], writes=[pfB])
            S_.op("act", ACTF(PC[:], pf[:, 0:128].rearrange("p (h c) -> p h c", c=16)[:, :, 0], AF.Exp, scale=CDEC), reads=[pfB], writes=[PCB])
            for nm in ("at", "rt", "bt", "kt"):
                pb, pbB = pbs.next()
                for hp in range(8):
                    S_.op("pe", TR(pb[:, hp * 128:(hp + 1) * 128], tk[nm][:, hp * 128:(hp + 1) * 128], identb[:]), reads=[tkB[nm]], writes=[pbB])
                pb3 = pb[:].rearrange("p (k t) -> p k t", t=128)
                if nm in ("at", "rt"):
                    a_ = 0 if nm == "at" else 1
                    S_.op("act", CPY(AR[0:64, :, 0, a_, :], pb3[0:64, :, :]), reads=[pbB], writes=[ARB])
                    S_.op("dve", CPY(AR[64:128, :, 1, a_, :], pb3[64:128, :, :]), reads=[pbB], writes=[ARB])
                elif nm == "bt":
                    S_.op("act", CPY(BTf[:], pb3), reads=[pbB], writes=[BTfB])
                else:
                    S_.op("dve", CPY(KTf[:], pb3), reads=[pbB], writes=[KTfB])
            for hb_ in range(2):
                for pl in range(4):
                    hp = hb_ * 4 + pl
                    po, poB = pfs.next()
                    pk, pkB = pfs.next()
                    pn, pnB = pfs.next()
                    for half in range(2):
                        ar2 = AR[:, hp, half, :, :].rearrange("p a t -> p (a t)")
                        S_.op("pe", MM(po[:, half * 256:(half + 1) * 256], BTf[:, hp, :], ar2), reads=[BTfB, ARB], writes=[poB])
                        S_.op("pe", MM(pk[:, half * 256:(half + 1) * 256], KTf[:, hp, :], ar2), reads=[KTfB, ARB], writes=[pkB])
                        S_.op("pe", MM(pn[:, half * 128:(half + 1) * 128], AR[:, hp, half, 0, :], BTf[:, hp, :]), reads=[BTfB, ARB], writes=[pnB])
                    po3 = po[:, :].rearrange("p (h c) -> p h c", c=256)
                    pk3 = pk[:, :].rearrange("p (h c) -> p h c", c=256)
                    pn3 = pn[:, 0:256].rearrange("p (h c) -> p h c", c=128)
                    ms, mi = bc_mid(mar[:, 0:128], 2), bc_mid(mar[:, 128:256], 2)
                    S_.op("dve", TT(Nm[:, 2 * pl:2 * pl + 2, :], po3[:, :, 0:128], ms, ALU.mult), reads=[poB], writes=[NmB[pl]])
                    S_.op("dve", TT(MrbT[:, 2 * hp:2 * hp + 2, :], po3[:, :, 128:256], mi, ALU.mult), reads=[poB], writes=[MrbB])
                    S_.op("dve", TT(MakT[:, 2 * hp:2 * hp + 2, :], pk3[:, :, 0:128], ms, ALU.mult), reads=[pkB], writes=[MakB])
                    S_.op("dve", TT(MrkT[:, 2 * hp:2 * hp + 2, :], pk3[:, :, 128:256], mi, ALU.mult), reads=[pkB], writes=[MrkB])
                    S_.op("dve", TT(NT[:, 2 * pl:2 * pl + 2, :], pn3, bc_mid(mlo[:], 2), ALU.mult), reads=[pnB], writes=[NTB[pl]])
                for p2 in range(4):
                    S_.op("pool", TT(GQ[0][:, 2 * p2:2 * p2 + 2, 0, :], Nm[:, 2 * p2:2 * p2 + 2, :], bc_mid(identb[:], 2), ALU.add),
                          reads=[NmB[p2]], writes=[GQB[0][p2]])
                for q4 in range(2):
                    pq, pqB = pfs.next()
                    pt_, ptB = pfs.next()
                    for hl in range(4 * q4, 4 * q4 + 4):
                        o_ = slice((hl % 4) * 128, (hl % 4 + 1) * 128)
                        S_.op("pe", MM(pq[:, o_], NT[:, hl, :], Nm[:, hl, :]), reads=[NTB[hl // 2], NmB[hl // 2]], writes=[pqB])
                        S_.op("pe", MM(pt_[:, o_], Nm[:, hl, :], NT[:, hl, :]), reads=[NTB[hl // 2], NmB[hl // 2]], writes=[ptB])
                    S_.op("act", CPY(GQ[0][:, 4 * q4:4 * q4 + 4, 1, :], pq[:, :].rearrange("p (h c) -> p h c", c=128)), reads=[pqB], writes=[GQB[0][2 * q4], GQB[0][2 * q4 + 1]])
                    S_.op("act", CPY(QTt[0][:, 4 * q4:4 * q4 + 4, :], pt_[:, :].rearrange("p (h c) -> p h c", c=128)), reads=[ptB], writes=[QTB[0][q4]])
                for i in range(1, 7):
                    ci, ni = (i - 1) % 2, i % 2
                    last = (i == 6)
                    if not last:
                        for p2 in range(4):
                            pa, paB = pfs.next()
                            for hl in (2 * p2, 2 * p2 + 1):
                                o0 = (hl % 2) * 256
                                rd = [QTB[ci][hl // 4], GQB[ci][hl // 2]]
                                S_.op("pe", MM(pa[:, o0 + 128:o0 + 256], QTt[ci][:, hl, :], GQ[ci][:, hl, 1, :], True, True), reads=rd, writes=[paB])
                                S_.op("pe", MM(pa[:, o0:o0 + 128], QTt[ci][:, hl, :], GQ[ci][:, hl, 0, :], True, False), reads=rd, writes=[paB])
                                S_.op("pe", MM(pa[:, o0:o0 + 128], identb[:], GQ[ci][:, hl, 0, :], False, True), reads=rd, writes=[paB])
                            S_.op("dve" if p2 % 2 == 0 else "act",
                                  CPY(GQ[ni][:, 2 * p2:2 * p2 + 2, :, :].rearrange("p h a t -> p h (a t)"), pa[:, :].rearrange("p (h c) -> p h c", c=256)),
                                  reads=[paB, GQB[ci][p2]], writes=[GQB[ni][p2]])
                        for q4 in range(2):
                            pt_, ptB = pfs.next()
                            for hl in range(4 * q4, 4 * q4 + 4):
                                S_.op("pe", MM(pt_[:, (hl % 4) * 128:(hl % 4 + 1) * 128], GQ[ci][:, hl, 1, :], QTt[ci][:, hl, :]),
                                      reads=[QTB[ci][hl // 4], GQB[ci][hl // 2]], writes=[ptB])
                            S_.op("act", CPY(QTt[ni][:, 4 * q4:4 * q4 + 4, :], pt_[:, :].rearrange("p (h c) -> p h c", c=128)), reads=[ptB], writes=[QTB[ni][q4]])
                    else:
                        for q4 in range(2):
                            pa, paB = pfs.next()
                            for hl in range(4 * q4, 4 * q4 + 4):
                                S_.op("pe", MM(pa[:, (hl % 4) * 128:(hl % 4 + 1) * 128], QTt[ci][:, hl, :], GQ[ci][:, hl, 0, :]),
                                      reads=[QTB[ci][hl // 4], GQB[ci][hl // 2]], writes=[paB])
                            S_.op("dve", TT(G7[:, hb_ * 8 + 4 * q4:hb_ * 8 + 4 * q4 + 4, :], pa[:, :].rearrange("p (h c) -> p h c", c=128),
                                            GQ[ci][:, 4 * q4:4 * q4 + 4, 0, :], ALU.add), reads=[paB, GQB[ci][2 * q4], GQB[ci][2 * q4 + 1]], writes=[G7B])

            if c + 1 < NCHL:
                interleave(st78(cx), st12(cxs[c + 1]))
            else:
                run_all(st78(cx))
        S_.barrier()
        S_.emit()
```
